# Optimizing a Trainium2 kernel written in Bass

```python
import math
import jax, jax.numpy as jnp
from jax import lax
import numpy as np


D_MODEL = 1024
BATCH = 16
SEQ = 2048
DEPTH = 1
DEC_BATCH = 32
DEC_SEQ = 2048
PAST_LEN = 128

N_META = 16
CONV_CH = D_MODEL // 2
CONV_K = 31
MLA_HEADS = 8
QK_NOPE = 64
QK_ROPE = 32
V_DIM = 64
Q_LORA = D_MODEL // 4
KV_LORA = D_MODEL // 4
D_FF = ((8 * D_MODEL // 3 + 255) // 256) * 256
Q_BLOCK = 128
ROPE_THETA = 10000.0
EPS = 1e-6
IN_WIDTH = 2 * CONV_CH + Q_LORA + KV_LORA + QK_ROPE
MIX_WIDTH = CONV_CH + MLA_HEADS * V_DIM
ATTN_SCALE = 1.0 / math.sqrt(QK_NOPE + QK_ROPE)

kernel_name = 'hybrid_conformer_mla_encoder'


def _rms(x, g):
    x32 = x.astype(jnp.float32)
    y = x32 * lax.rsqrt(jnp.mean(x32 * x32, axis=-1, keepdims=True) + EPS)
    return (y * g.astype(jnp.float32)).astype(x.dtype)


def _layernorm(x, g, b):
    x32 = x.astype(jnp.float32)
    mu = jnp.mean(x32, axis=-1, keepdims=True)
    var = jnp.mean(jnp.square(x32 - mu), axis=-1, keepdims=True)
    y = (x32 - mu) * lax.rsqrt(var + EPS)
    return (y * g.astype(jnp.float32) + b.astype(jnp.float32)).astype(x.dtype)


def _rope_tables(length, dtype):
    inv = 1.0 / (ROPE_THETA ** (jnp.arange(0, QK_ROPE, 2, dtype=jnp.float32) / QK_ROPE))
    ang = jnp.arange(length, dtype=jnp.float32)[:, None] * inv[None, :]
    return jnp.cos(ang).astype(dtype), jnp.sin(ang).astype(dtype)


def _rope(x, cos, sin):
    x1, x2 = jnp.split(x, 2, axis=-1)
    return jnp.concatenate([x1 * cos - x2 * sin, x2 * cos + x1 * sin], axis=-1)


def _attend_block(qn, qr, kn, kr, v):
    s = jnp.einsum('bqhd,bkhd->bhqk', qn, kn) + jnp.einsum('bqhd,bkd->bhqk', qr, kr)
    p = jax.nn.softmax(s.astype(jnp.float32) * ATTN_SCALE, axis=-1).astype(v.dtype)
    return jnp.einsum('bhqk,bkhd->bqhd', p, v)


def _mla_attention(qn, qr, kn, kr, v):
    b, l, h, _ = qn.shape
    s = l - N_META
    nb = s // Q_BLOCK
    out_meta = _attend_block(qn[:, :N_META], qr[:, :N_META], kn, kr, v)
    qn_r = qn[:, N_META:].reshape(b, nb, Q_BLOCK, h, QK_NOPE).transpose(1, 0, 2, 3, 4)
    qr_r = qr[:, N_META:].reshape(b, nb, Q_BLOCK, h, QK_ROPE).transpose(1, 0, 2, 3, 4)
    out_r = lax.map(lambda qs: _attend_block(qs[0], qs[1], kn, kr, v), (qn_r, qr_r))
    out_r = out_r.transpose(1, 0, 2, 3, 4).reshape(b, s, h, V_DIM)
    return jnp.concatenate([out_meta, out_r], axis=1)


def _depthwise_conv(u, w, bias):
    c = u.shape[-1]
    y = lax.conv_general_dilated(
        u, w[:, None, :].astype(u.dtype), window_strides=(1,),
        padding=[((CONV_K - 1) // 2, (CONV_K - 1) // 2)],
        dimension_numbers=('NWC', 'WIO', 'NWC'), feature_group_count=c)
    return y + bias


def _encode(x, meta_tokens, attn_norm_g, w_in, q_norm_g, w_q_up, kv_norm_g, w_kv_up,
            conv_dw_w, conv_dw_b, conv_ln_g, conv_ln_b, w_out, ffn_norm_g,
            w_gate, w_up, w_down, final_norm_g):
    b = x.shape[0]
    meta = jnp.broadcast_to(meta_tokens[None].astype(x.dtype), (b, N_META, D_MODEL))
    h = jnp.concatenate([meta, x], axis=1)
    l = h.shape[1]
    cos, sin = _rope_tables(l, h.dtype)
    for i in range(DEPTH):
        hn = _rms(h, attn_norm_g[i])
        z = hn @ w_in[i]
        c_val, c_gate, cq, ckv, k_rope = jnp.split(
            z, [CONV_CH, 2 * CONV_CH, 2 * CONV_CH + Q_LORA, 2 * CONV_CH + Q_LORA + KV_LORA], axis=-1)
        u = c_val * jax.nn.sigmoid(c_gate)
        u = _depthwise_conv(u, conv_dw_w[i], conv_dw_b[i])
        u = jax.nn.silu(_layernorm(u, conv_ln_g[i], conv_ln_b[i]))
        q = (_rms(cq, q_norm_g[i]) @ w_q_up[i]).reshape(b, l, MLA_HEADS, QK_NOPE + QK_ROPE)
        q_nope, q_rope = jnp.split(q, [QK_NOPE], axis=-1)
        kv = (_rms(ckv, kv_norm_g[i]) @ w_kv_up[i]).reshape(b, l, MLA_HEADS, QK_NOPE + V_DIM)
        k_nope, v = jnp.split(kv, [QK_NOPE], axis=-1)
        q_rope = _rope(q_rope, cos[None, :, None, :], sin[None, :, None, :])
        k_rope = _rope(k_rope, cos[None], sin[None])
        o = _mla_attention(q_nope, q_rope, k_nope, k_rope, v).reshape(b, l, MLA_HEADS * V_DIM)
        h = h + jnp.concatenate([u, o], axis=-1) @ w_out[i]
        hf = _rms(h, ffn_norm_g[i])
        h = h + (jax.nn.silu(hf @ w_gate[i]) * (hf @ w_up[i])) @ w_down[i]
    return _rms(h, final_norm_g)[:, N_META:]


def setup_inputs(seed: int = 0) -> dict:
    key = jax.random.key(seed)
    ks = jax.random.split(key, 20)
    f32 = jnp.float32

    def nrm(k, shape, scale):
        return jax.random.normal(k, shape, f32) * scale

    def gain(k, shape):
        return 1.0 + 0.02 * jax.random.normal(k, shape, f32)

    return {
        'x_prompt': nrm(ks[0], (BATCH, SEQ, D_MODEL), 1.0),
        'x_sample': nrm(ks[1], (DEC_BATCH, DEC_SEQ, D_MODEL), 1.0),
        'meta_tokens': nrm(ks[2], (N_META, D_MODEL), 1.0),
        'attn_norm_g': gain(ks[3], (DEPTH, D_MODEL)),
        'w_in': nrm(ks[4], (DEPTH, D_MODEL, IN_WIDTH), D_MODEL ** -0.5),
        'q_norm_g': gain(ks[5], (DEPTH, Q_LORA)),
        'w_q_up': nrm(ks[6], (DEPTH, Q_LORA, MLA_HEADS * (QK_NOPE + QK_ROPE)), Q_LORA ** -0.5),
        'kv_norm_g': gain(ks[7], (DEPTH, KV_LORA)),
        'w_kv_up': nrm(ks[8], (DEPTH, KV_LORA, MLA_HEADS * (QK_NOPE + V_DIM)), KV_LORA ** -0.5),
        'conv_dw_w': nrm(ks[9], (DEPTH, CONV_K, CONV_CH), CONV_K ** -0.5),
        'conv_dw_b': nrm(ks[10], (DEPTH, CONV_CH), 0.02),
        'conv_ln_g': gain(ks[11], (DEPTH, CONV_CH)),
        'conv_ln_b': nrm(ks[12], (DEPTH, CONV_CH), 0.02),
        'w_out': nrm(ks[13], (DEPTH, MIX_WIDTH, D_MODEL), MIX_WIDTH ** -0.5),
        'ffn_norm_g': gain(ks[14], (DEPTH, D_MODEL)),
        'w_gate': nrm(ks[15], (DEPTH, D_MODEL, D_FF), D_MODEL ** -0.5),
        'w_up': nrm(ks[16], (DEPTH, D_MODEL, D_FF), D_MODEL ** -0.5),
        'w_down': nrm(ks[17], (DEPTH, D_FF, D_MODEL), D_FF ** -0.5),
        'final_norm_g': gain(ks[18], (D_MODEL,)),
    }


def reference(x_prompt, x_sample, meta_tokens, attn_norm_g, w_in, q_norm_g, w_q_up, kv_norm_g,
              w_kv_up, conv_dw_w, conv_dw_b, conv_ln_g, conv_ln_b, w_out, ffn_norm_g,
              w_gate, w_up, w_down, final_norm_g):
    y_prompt = _encode(x_prompt, meta_tokens, attn_norm_g, w_in, q_norm_g, w_q_up, kv_norm_g,
                       w_kv_up, conv_dw_w, conv_dw_b, conv_ln_g, conv_ln_b, w_out, ffn_norm_g,
                       w_gate, w_up, w_down, final_norm_g)
    y_sample = _encode(x_sample, meta_tokens, attn_norm_g, w_in, q_norm_g, w_q_up, kv_norm_g,
                       w_kv_up, conv_dw_w, conv_dw_b, conv_ln_g, conv_ln_b, w_out, ffn_norm_g,
                       w_gate, w_up, w_down, final_norm_g)
    return (y_prompt, y_sample)
```

```python
import math
from contextlib import ExitStack

import numpy as np
import ml_dtypes
import concourse.bass as bass
import concourse.mybir as mybir
from concourse.bass_utils import run_bass_kernel_spmd

F32 = mybir.dt.float32
BF16 = mybir.dt.bfloat16
AF = mybir.ActivationFunctionType
ALU = mybir.AluOpType

D = 1024
NMETA = 16
CCH = 512
CK = 31
NH = 8
DN = 64
DR = 32
DV = 64
QL = 256
DFF = 2816
NJ = DFF // 128
INW = 1568
EPS = 1e-6
ATTN_SCALE = 1.0 / math.sqrt(DN + DR)
N_CORES = 8


class Buf:
    __slots__ = ("name", "writers", "readers", "excl")

    def __init__(self, name="", excl=False):
        self.name = name
        self.excl = excl
        self.writers = {}
        self.readers = {}


def _merge(d, tok):
    k = tok[2]
    if k not in d or d[k][1] < tok[1]:
        d[k] = tok


class Prog:
    ENGS = ("pe", "act", "dve", "pool", "sp")

    def __init__(self, nc, stack):
        self.nc = nc
        self.stack = stack
        self.eng = {"pe": nc.tensor, "act": nc.scalar, "dve": nc.vector, "pool": nc.gpsimd, "sp": nc.sync}
        self.sem = {e: stack.enter_context(nc.semaphore("s_" + e)) for e in self.ENGS}
        self.cnt = {e: 0 for e in self.ENGS}
        self.waited = {e: {} for e in self.ENGS}
        self.dma_sems = []
        self.after_dve = None

    def _waits_for(self, eng, toks):
        best = {}
        for t in toks:
            sem, val, key, src = t
            if src == "pe" and eng == "pe":
                continue
            if key not in best or best[key][1] < val:
                best[key] = (sem, val)
        w = self.waited[eng]
        out = []
        for key, (sem, val) in best.items():
            if w.get(key, -1) >= val:
                continue
            w[key] = val
            out.append((sem, val))
        return out

    def _deps(self, reads, writes, extra):
        toks = list(extra)
        for b in reads:
            toks += list(b.writers.values())
            if b.excl:
                toks += list(b.readers.values())
        for b in writes:
            toks += list(b.writers.values())
            toks += list(b.readers.values())
        return toks

    def _emit(self, eng, waits, fns, inc, each):
        e = self.eng[eng]
        for sem, val in waits:
            e.wait_ge(sem, val)
        n = len(fns)
        for i, f in enumerate(fns):
            ins = f(e)
            if each or i == n - 1:
                ins.then_inc(inc[0], inc[1])

    def _record(self, tok, reads, writes):
        for b in reads:
            _merge(b.readers, tok)
        for b in writes:
            b.writers = {tok[2]: tok}
            b.readers = {}

    def task(self, eng, fns, reads=(), writes=(), extra=()):
        if callable(fns):
            fns = [fns]
        waits = self._waits_for(eng, self._deps(reads, writes, extra))
        self.cnt[eng] += 1
        tok = (self.sem[eng], self.cnt[eng], eng, eng)
        self._emit(eng, waits, fns, (self.sem[eng], 1), False)
        self._record(tok, reads, writes)
        hook = self.after_dve
        if eng == "dve" and hook is not None:
            self.after_dve = None
            hook()
            self.after_dve = hook
        return tok

    def new_dma_sem(self, name):
        s = self.stack.enter_context(self.nc.semaphore(name))
        st = {"sem": s, "val": 0, "key": "dma%d_%s" % (len(self.dma_sems), name)}
        self.dma_sems.append(st)
        return st

    def dma(self, eng, dsem, fns, reads=(), writes=(), extra=()):
        if callable(fns):
            fns = [fns]
        waits = self._waits_for(eng, self._deps(reads, writes, extra))
        dsem["val"] += 16 * len(fns)
        tok = (dsem["sem"], dsem["val"], dsem["key"], None)
        self._emit(eng, waits, fns, (dsem["sem"], 16), True)
        self._record(tok, reads, writes)
        return tok

    def wait_all(self, eng, toks):
        e = self.eng[eng]
        for sem, val in self._waits_for(eng, toks):
            e.wait_ge(sem, val)

    def barrier(self):
        toks = [(self.sem[e], self.cnt[e], e, e) for e in self.ENGS if self.cnt[e] > 0]
        toks += [(d["sem"], d["val"], d["key"], None) for d in self.dma_sems if d["val"] > 0]
        for e in self.ENGS:
            self.wait_all(e, [t for t in toks if t[3] != e])


def build(NSEQ, S, stop=99):
    assert S % 512 == 0
    NT = S // 128
    NB = S // 512
    L = S + NMETA
    NKT = NT + 1
    UPW = S + NMETA + 30

    nc = bass.Bass("TRN2", target_bir_lowering=False)

    def din(name, shape, dt=F32):
        return nc.dram_tensor(name, list(shape), dt, kind="ExternalInput").ap()

    x = din("x", [NSEQ, S, D])
    meta = din("meta", [NMETA, D])
    w_in_l = din("w_in_l", [128, 8, INW])
    wq_l = din("wq_l", [128, 2, 768])
    wkv_l = din("wkv_l", [128, 2, 1024])
    w_out_l = din("w_out_l", [128, 8, D])
    wg_l = din("wg_l", [NJ, 128, 8, 128])
    wu_l = din("wu_l", [NJ, 128, 8, 128])
    wd_l = din("wd_l", [NJ, 128, D])
    gattn_l = din("gattn_l", [128, 8])
    gq_l = din("gq_l", [128, 2])
    gkv_l = din("gkv_l", [128, 2])
    gffn_l = din("gffn_l", [128, 8])
    cw_l = din("cw_l", [128, 4, CK])
    cb_l = din("cb_l", [128, 4])
    lng_l = din("lng_l", [128, 4])
    lnb_l = din("lnb_l", [128, 4])
    gfin_l = din("gfin_l", [128, D])
    ident_l = din("ident_l", [128, 128], BF16)
    cs_tok_l = din("cs_tok_l", [128, NT, 32])
    sn_tok_l = din("sn_tok_l", [128, NT, 32])
    cs_meta_l = din("cs_meta_l", [NMETA, 32])
    sn_meta_l = din("sn_meta_l", [NMETA, 32])
    out = nc.dram_tensor("out", [NSEQ, S, D], F32, kind="ExternalOutput").ap()

    win_s = nc.dram_tensor("win_s", [128, 8 * INW], BF16).ap()
    wout_s = nc.dram_tensor("wout_s", [128, 8 * D], BF16).ap()
    wg_s = nc.dram_tensor("wg_s", [NJ, 128, 1024], BF16).ap()
    wu_s = nc.dram_tensor("wu_s", [NJ, 128, 1024], BF16).ap()
    wd_s = nc.dram_tensor("wd_s", [NJ, 128, D], BF16).ap()
    upre_d = nc.dram_tensor("upre_d", [4, 128, UPW], F32).ap()

    with ExitStack() as st:
        P = Prog(nc, st)

        uid = [0]

        def sbuf(stack, name, shape, dt):
            uid[0] += 1
            return stack.enter_context(nc.sbuf_tensor("%s_%d" % (name, uid[0]), list(shape), dt))

        pp = [st.enter_context(nc.psum_tensor("pp%d" % i, [128, 1024], F32)) for i in range(4)]
        ppb = [[Buf("pp%d_%d" % (i, h), True) for h in range(2)] for i in range(4)]

        ident = sbuf(st, "ident", [128, 128], BF16)
        ones32 = sbuf(st, "ones32", [128, 128], F32)
        neghalf = sbuf(st, "neghalf", [128, 8], F32)
        zpad = sbuf(st, "zpad", [128, 4, 16], F32)
        gattn = sbuf(st, "gattn", [128, 8], F32)
        gq = sbuf(st, "gq", [128, 2], F32)
        gkv = sbuf(st, "gkv", [128, 2], F32)
        gffn = sbuf(st, "gffn", [128, 8], F32)
        cw = sbuf(st, "cw", [128, 4, CK], F32)
        cb = sbuf(st, "cb", [128, 4], F32)
        lng = sbuf(st, "lng", [128, 4], F32)
        lnb = sbuf(st, "lnb", [128, 4], F32)
        gfin = sbuf(st, "gfin", [128, D], F32)
        cs_tok = sbuf(st, "cs_tok", [128, NT, 32], F32)
        sn_tok = sbuf(st, "sn_tok", [128, NT, 32], F32)
        cs_meta = sbuf(st, "cs_meta", [NMETA, 32], F32)
        sn_meta = sbuf(st, "sn_meta", [NMETA, 32], F32)
        wq_sb = sbuf(st, "wq_sb", [128, 2, 768], BF16)
        wkv_sb = sbuf(st, "wkv_sb", [128, 2, 1024], BF16)
        KT = sbuf(st, "KT", [128, NH, L], BF16)
        Vsb = sbuf(st, "Vsb", [128, NKT, NH * (DV + 1) + 64], BF16)
        uTb = [sbuf(st, "uTb%d" % i, [128, 4, 512], BF16) for i in range(2)]
        cnTq = sbuf(st, "cnTq", [128, 2, S], BF16)
        upre_meta = sbuf(st, "upre_meta", [128, 4, NMETA], F32)
        stats = sbuf(st, "stats", [128, 8, 8], F32)

        B_const = Buf("const")
        B_wq = Buf("wq")
        B_wkv = Buf("wkv")
        B_KT = [Buf("KT%d" % i) for i in range(NKT)]
        B_V = [Buf("V%d" % i) for i in range(NKT)]
        B_uTb = [Buf("uTb0"), Buf("uTb1")]
        B_upd = [[Buf("upd%d_%d" % (g, c)) for c in range(4)] for g in range(NB)]
        B_upd_pad = Buf("upd_pad")
        B_cnTq = [Buf("cnTq%d" % i) for i in range(NT)]
        B_upm = Buf("upre_meta")
        B_stats = [Buf("stats%d" % i) for i in range(8)]
        B_win_s = Buf("win_s")
        B_win_parts = [Buf("win_s0"), Buf("win_s1")]
        B_wg_parts = [Buf("wg_s%d" % i) for i in range(4)]
        B_wu_parts = [Buf("wu_s%d" % i) for i in range(4)]
        B_wout_s = Buf("wout_s")
        B_wg_s = Buf("wg_s")
        B_wu_s = Buf("wu_s")
        B_wd_s = Buf("wd_s")
        stat_rr = [0]

        def next_stat():
            i = stat_rr[0] % 8
            stat_rr[0] += 1
            return stats[:, i, :], B_stats[i]

        dconst = P.new_dma_sem("dconst")
        dwq = P.new_dma_sem("dwq")
        dwkv = P.new_dma_sem("dwkv")

        consts = [(ident, ident_l), (gattn, gattn_l), (gq, gq_l), (gkv, gkv_l), (gffn, gffn_l), (cw, cw_l),
                  (cb, cb_l), (lng, lng_l), (lnb, lnb_l), (gfin, gfin_l), (cs_tok, cs_tok_l), (sn_tok, sn_tok_l),
                  (cs_meta, cs_meta_l), (sn_meta, sn_meta_l)]
        P.dma("sp", dconst, [(lambda e, d=d, s=s: e.dma_start(out=d[:], in_=s)) for d, s in consts], writes=[B_const])
        B_ones = Buf("ones")
        P.task("pool", [lambda e: e.memset(ones32[:], 1.0), lambda e: e.memset(neghalf[:], -0.5), lambda e: e.memset(zpad[:], 0.0),
                        lambda e: e.memset(Vsb[:], 0.0),
                        lambda e: e.memset(KT[96:128, :, :], 0.0)], writes=[B_ones] + B_V + B_KT)
        P.task("pool", lambda e: e.memset(
            Vsb[:, :, 0:NH * (DV + 1)].rearrange("p k (h d) -> p k h d", h=NH)[:, :, :, DV:DV + 1], 1.0), writes=B_V)

        def rstd_from_ssq(ssq_ap, out_ap, n, bst, np_=128, w=1):
            P.task("dve", lambda e: e.tensor_scalar(out_ap, ssq_ap, 1.0 / n, EPS, op0=ALU.mult, op1=ALU.add),
                   reads=[bst], writes=[bst])
            P.task("pool", lambda e: e.tensor_tensor(out_ap, out_ap, neghalf[0:np_, 0:w], op=ALU.pow),
                   reads=[bst, B_ones], writes=[bst])

        with ExitStack() as ps_:
            HW_ = 4 * INW
            st32 = [sbuf(ps_, "st32_%d" % i, [128, HW_], F32) for i in range(2)]
            stb = [sbuf(ps_, "stb_%d" % i, [128, HW_], BF16) for i in range(2)]
            B32 = [Buf("st32a"), Buf("st32b")]
            Bb = [Buf("stba"), Buf("stbb")]
            dst32 = [P.new_dma_sem("dst32a"), P.new_dma_sem("dst32b")]
            dstb = [P.new_dma_sem("dstba"), P.new_dma_sem("dstbb")]
            dcast = P.new_dma_sem("dcast")
            bi = [0]

            def nxt():
                i = bi[0] % 2
                bi[0] += 1
                return i

            for hf in range(2):
                i = nxt()
                P.dma("sp", dst32[i], lambda e, i=i, hf=hf: e.dma_start(
                    out=st32[i][:], in_=w_in_l[:, 4 * hf:4 * hf + 4, :].rearrange("p k n -> p (k n)")), writes=[B32[i]])
                P.task("dve", [(lambda e, kk=kk, i=i, hf=hf: e.tensor_scalar(
                    stb[i][:, kk * INW:(kk + 1) * INW], st32[i][:, kk * INW:(kk + 1) * INW],
                    gattn[:, 4 * hf + kk:4 * hf + kk + 1], None, op0=ALU.mult)) for kk in range(4)],
                    reads=[B32[i], B_const], writes=[Bb[i]])
                P.dma("sp", dstb[i], lambda e, i=i, hf=hf: e.dma_start(
                    out=win_s[:, 4 * hf * INW:(4 * hf + 4) * INW], in_=stb[i][:]), reads=[Bb[i]], writes=[B_win_parts[hf]])
            i = nxt()
            P.dma("sp", dst32[i], [lambda e, i=i: e.dma_start(out=st32[i][:, 0:1536], in_=wq_l.rearrange("p k n -> p (k n)")),
                                   lambda e, i=i: e.dma_start(out=st32[i][:, 1536:3584], in_=wkv_l.rearrange("p k n -> p (k n)"))],
                  writes=[B32[i]])
            P.task("dve", [(lambda e, k=k, i=i: e.tensor_scalar(wq_sb[:, k, :], st32[i][:, k * 768:(k + 1) * 768],
                                                                gq[:, k:k + 1], None, op0=ALU.mult)) for k in range(2)] +
                          [(lambda e, k=k, i=i: e.tensor_scalar(wkv_sb[:, k, :], st32[i][:, 1536 + k * 1024:1536 + (k + 1) * 1024],
                                                                gkv[:, k:k + 1], None, op0=ALU.mult)) for k in range(2)],
                   reads=[B32[i], B_const], writes=[B_wq, B_wkv])
            P.dma("pool", dcast, [(lambda e, k=k: e.dma_start(out=wout_s[:, k * D:(k + 1) * D], in_=w_out_l[:, k, :]))
                                  for k in range(8)], writes=[B_wout_s])
            P.dma("pool", dcast, [(lambda e, j=j: e.dma_start(out=wd_s[j], in_=wd_l[j])) for j in range(NJ)], writes=[B_wd_s])
            JB = 6
            for (src, dst, bparts) in ((wg_l, wg_s, B_wg_parts), (wu_l, wu_s, B_wu_parts)):
                for j0 in range(0, NJ, JB):
                    nj = min(JB, NJ - j0)
                    bdst = bparts[j0 // JB]
                    i = nxt()
                    P.dma("sp", dst32[i], lambda e, src=src, j0=j0, nj=nj, i=i: e.dma_start(
                        out=st32[i][:, 0:nj * 1024].rearrange("p (j q) -> p j q", j=nj),
                        in_=src[j0:j0 + nj].rearrange("j p k m -> p j (k m)")), writes=[B32[i]])
                    P.task("dve", [(lambda e, k=k, nj=nj, i=i: e.tensor_scalar(
                        stb[i][:, 0:nj * 1024].rearrange("p (j k m) -> p j k m", j=nj, k=8)[:, :, k, :],
                        st32[i][:, 0:nj * 1024].rearrange("p (j k m) -> p j k m", j=nj, k=8)[:, :, k, :],
                        gffn[:, k:k + 1], None, op0=ALU.mult)) for k in range(8)],
                        reads=[B32[i], B_const], writes=[Bb[i]])
                    P.dma("sp", dstb[i], lambda e, dst=dst, j0=j0, nj=nj, i=i: e.dma_start(
                        out=dst[j0:j0 + nj].rearrange("j p q -> p j q"),
                        in_=stb[i][:, 0:nj * 1024].rearrange("p (j q) -> p j q", j=nj)), reads=[Bb[i]], writes=[bdst])
            P.barrier()

        def conv_emit(C, k, c):
            src = C["uwin"][:, c, k:k + 512]
            acc = C["acc"]
            if k == 0:
                P.task("dve", lambda e: e.tensor_scalar(
                    acc[:, c, :], src, cw[:, c, 0:1], cb[:, c:c + 1], op0=ALU.mult, op1=ALU.add),
                    reads=[C["Buwin"], B_const], writes=[C["Bacc"][c]])
            else:
                P.task("dve", lambda e: e.scalar_tensor_tensor(
                    acc[:, c, :], src, cw[:, c, k:k + 1], acc[:, c, :], op0=ALU.mult, op1=ALU.add),
                    reads=[C["Buwin"], B_const], writes=[C["Bacc"][c]])

        def win_load(C, b):
            o0 = NMETA + b * 512
            deps = [B_upd_pad] + [B_upd[g][c] for g in range(NB) if b - 1 <= g <= b + 1 for c in range(4)]
            P.dma("sp", C["dwin"], lambda e: e.dma_start(
                out=C["uwin"][:], in_=upre_d[:, :, o0:o0 + 542].rearrange("c p n -> p c n")),
                reads=deps, writes=[C["Buwin"]])

        def ln_emit(C, dst, bdst, spi):
            acc, csq, mean, rstd = C["acc"], C["csq"], C["lnm"], C["lnr"]
            Bacc, Bcsq, Blnm, Blnr = C["Bacc"], C["Bcsq"], C["Blnm"], C["Blnr"]
            for c in range(4):
                sl = c % 2
                P.task("act", lambda e, c=c, sl=sl: e.activation(out=csq[sl][:], in_=acc[:, c, :], func=AF.Square),
                       reads=[Bacc[c]], writes=[Bcsq[sl]])
                P.task("pe", lambda e, c=c: e.matmul(pp[spi][:, 0:512], lhsT=ones32[:], rhs=acc[:, c, :], start=(c == 0), stop=(c == 3)),
                       reads=[Bacc[c], B_ones], writes=[ppb[spi][0]])
                P.task("pe", lambda e, c=c, sl=sl: e.matmul(pp[spi][:, 512:1024], lhsT=ones32[:], rhs=csq[sl][:], start=(c == 0), stop=(c == 3)),
                       reads=[Bcsq[sl], B_ones], writes=[ppb[spi][1]])
            P.task("dve", lambda e: e.tensor_scalar(mean[:], pp[spi][:, 0:512], 1.0 / CCH, None, op0=ALU.mult),
                   reads=[ppb[spi][0]], writes=[Blnm])
            P.task("dve", lambda e: e.tensor_tensor(rstd[:], mean[:], mean[:], op=ALU.mult), reads=[Blnm], writes=[Blnr])
            P.task("dve", lambda e: e.scalar_tensor_tensor(rstd[:], pp[spi][:, 512:1024], 1.0 / CCH, rstd[:], op0=ALU.mult, op1=ALU.subtract),
                   reads=[ppb[spi][1]], writes=[Blnr])
            P.task("dve", lambda e: e.tensor_scalar(rstd[:], rstd[:], EPS, None, op0=ALU.add), reads=[], writes=[Blnr])
            P.task("act", lambda e: e.activation(out=rstd[:], in_=rstd[:], func=AF.Sqrt), reads=[], writes=[Blnr])
            P.task("dve", lambda e: e.reciprocal(rstd[:], rstd[:]), reads=[], writes=[Blnr])
            for c in range(4):
                P.task("dve", lambda e, c=c: e.tensor_tensor(acc[:, c, :], acc[:, c, :], mean[:], op=ALU.subtract),
                       reads=[Blnm], writes=[Bacc[c]])
                P.task("dve", lambda e, c=c: e.tensor_tensor(acc[:, c, :], acc[:, c, :], rstd[:], op=ALU.mult),
                       reads=[Blnr], writes=[Bacc[c]])
                P.task("act", lambda e, c=c: e.activation(out=dst[:, c, :], in_=acc[:, c, :], func=AF.Silu,
                                                          bias=lnb[:, c:c + 1], scale=lng[:, c:c + 1]),
                       reads=[Bacc[c], B_const], writes=[bdst])

        def allocC(stack, tag, csq=None):
            C = {}
            C["uwin"] = sbuf(stack, "uwin" + tag, [128, 4, 542], F32)
            C["acc"] = sbuf(stack, "acc" + tag, [128, 4, 512], F32)
            C["lnm"] = sbuf(stack, "lnm" + tag, [128, 512], F32)
            C["lnr"] = sbuf(stack, "lnr" + tag, [128, 512], F32)
            C["csq"] = csq if csq is not None else [sbuf(stack, "csq%s%d" % (tag, i), [128, 512], F32) for i in range(2)]
            C["Buwin"], C["Blnm"], C["Blnr"] = Buf(), Buf(), Buf()
            C["Bacc"] = [Buf() for _ in range(4)]
            C["Bcsq"] = [Buf(), Buf()]
            C["dwin"] = DS["duwin"]
            return C

        def phaseA(seq, A, meta_pass):
            w_in_sb, xt, xs, hnT, th, ust, cn, ckvT, ktm, krt = (A[k] for k in
                ("w_in_sb", "xt", "xs", "hnT", "th", "ust", "cn", "ckvT", "ktm", "krt"))
            Bx, Bxs, BhnT, Bth, Bup, Bcn, BckvT, Bktm_n, Bktm_r, Bkrt, Bwin = (A[k] for k in
                ("Bx", "Bxs", "BhnT", "Bth", "Bup", "Bcn", "BckvT", "Bktm_n", "Bktm_r", "Bkrt", "Bwin"))
            ngroups = 1 if meta_pass else NB
            np_ = NMETA if meta_pass else 128
            ncol = NMETA if meta_pass else 512

            def tiles_of(g):
                return [None] if meta_pass else list(range(4 * g, 4 * g + 4))

            gstat = {}

            def front(g):
                srow, bst = next_stat()
                gstat[g] = (srow, bst)
                tl_n = len(tiles_of(g))
                for tl, t in enumerate(tiles_of(g)):
                    src = meta if meta_pass else x[seq, t * 128:(t + 1) * 128, :]
                    P.dma("sp", A["dx"][tl], lambda e, tl=tl, src=src: e.dma_start(out=xt[tl][0:np_, :], in_=src),
                          writes=[Bx[tl]])
                    sl = tl % 2
                    P.task("act", lambda e, tl=tl, sl=sl, srow=srow: e.activation(
                        out=xs[sl][0:np_, :], in_=xt[tl][0:np_, :], func=AF.Square, accum_out=srow[0:np_, tl:tl + 1]),
                        reads=[Bx[tl]], writes=[Bxs[sl], bst])
                rstd_from_ssq(srow[0:np_, 0:tl_n], srow[0:np_, 4:4 + tl_n], D, bst, np_, tl_n)

            def mid(g):
                srow, bst = gstat[g]
                for tl, t in enumerate(tiles_of(g)):
                    sl = tl % 2
                    P.task("dve", lambda e, tl=tl, sl=sl, srow=srow: e.tensor_scalar(
                        xs[sl][0:np_, :], xt[tl][0:np_, :], srow[0:np_, 4 + tl:5 + tl], None, op0=ALU.mult),
                        reads=[Bx[tl], bst], writes=[Bxs[sl]])
                    mp = 2 + sl
                    P.task("pe", [(lambda e, c=c, sl=sl, mp=mp: e.matmul(
                        pp[mp][:, c * 128:c * 128 + np_], lhsT=xs[sl][0:np_, c * 128:(c + 1) * 128], rhs=ident[0:np_, 0:np_],
                        start=True, stop=True)) for c in range(8)],
                        reads=[Bxs[sl], B_const], writes=[ppb[mp][0], ppb[mp][1]])
                    if sl == 0:
                        P.task("act", lambda e, tl=tl, mp=mp: e.copy(
                            hnT[:, :, tl * 128:tl * 128 + np_],
                            pp[mp][:].rearrange("p (c t) -> p c t", c=8)[:, :, 0:np_]),
                            reads=[ppb[mp][0], ppb[mp][1]], writes=[BhnT[tl]])
                    else:
                        P.task("dve", lambda e, tl=tl, mp=mp: e.tensor_copy(
                            hnT[:, :, tl * 128:tl * 128 + np_],
                            pp[mp][:].rearrange("p (c t) -> p c t", c=8)[:, :, 0:np_]),
                            reads=[ppb[mp][0], ppb[mp][1]], writes=[BhnT[tl]])

            def valgate(g):
                for c in range(4):
                    vp = 1 + (c % 2)
                    P.task("pe", [(lambda e, k=k, c=c, vp=vp: e.matmul(
                        pp[vp][:, 512:512 + ncol], lhsT=w_in_sb[:, k, 512 + c * 128:512 + (c + 1) * 128], rhs=hnT[:, k, 0:ncol],
                        start=(k == 0), stop=(k == 7))) for k in range(8)],
                        reads=BhnT + [Bwin], writes=[ppb[vp][1]])
                    P.task("pe", [(lambda e, k=k, c=c, vp=vp: e.matmul(
                        pp[vp][:, 0:ncol], lhsT=w_in_sb[:, k, c * 128:(c + 1) * 128], rhs=hnT[:, k, 0:ncol],
                        start=(k == 0), stop=(k == 7))) for k in range(8)],
                        reads=BhnT + [Bwin], writes=[ppb[vp][0]])
                    sl = c % 2
                    P.task("act", lambda e, sl=sl, vp=vp: e.activation(out=th[sl][:, 0:ncol], in_=pp[vp][:, 512:512 + ncol], func=AF.Sigmoid),
                           reads=[ppb[vp][1]], writes=[Bth[sl]])
                    if meta_pass:
                        P.task("dve", lambda e, sl=sl, c=c, vp=vp: e.tensor_tensor(upre_meta[:, c, :], pp[vp][:, 0:ncol], th[sl][:, 0:ncol], op=ALU.mult),
                               reads=[ppb[vp][0], Bth[sl]], writes=[B_upm])
                    else:
                        c0 = 15 + NMETA + g * 512
                        P.task("dve", lambda e, sl=sl, vp=vp: e.tensor_tensor(ust[sl][:], pp[vp][:, 0:512], th[sl][:], op=ALU.mult),
                               reads=[ppb[vp][0], Bth[sl]], writes=[A["Bust"][sl]])
                        P.dma("sp", A["dust"][sl], lambda e, sl=sl, c=c, c0=c0: e.dma_start(out=upre_d[c, :, c0:c0 + 512], in_=ust[sl][:]),
                              reads=[A["Bust"][sl]], writes=[B_upd[g][c]])

            tstat = {}

            def cfront(g, tl):
                pz = 2 + (tl % 2)
                P.task("pe", [(lambda e, k=k: e.matmul(
                    pp[pz][0:np_, 0:512], lhsT=hnT[:, k, tl * 128:tl * 128 + np_], rhs=w_in_sb[:, k, 1024:1536],
                    start=(k == 0), stop=(k == 7))) for k in range(8)],
                    reads=[BhnT[tl], Bwin], writes=[ppb[pz][0]])
                P.task("pe", [(lambda e, k=k: e.matmul(
                    pp[pz][0:np_, 512:544], lhsT=hnT[:, k, tl * 128:tl * 128 + np_], rhs=w_in_sb[:, k, 1536:1568],
                    start=(k == 0), stop=(k == 7))) for k in range(8)],
                    reads=[BhnT[tl], Bwin], writes=[ppb[pz][1]])
                srow, bst = next_stat()
                tstat[(g, tl)] = (srow, bst)
                sl = tl % 2
                P.task("act", [lambda e: e.activation(
                    out=cn[sl][0:np_, 0:256], in_=pp[pz][0:np_, 0:256], func=AF.Square, accum_out=srow[0:np_, 0:1]),
                    lambda e: e.activation(
                    out=cn[sl][0:np_, 256:512], in_=pp[pz][0:np_, 256:512], func=AF.Square, accum_out=srow[0:np_, 1:2])],
                    reads=[ppb[pz][0]], writes=[Bcn[sl], bst])
                rstd_from_ssq(srow[0:np_, 0:2], srow[0:np_, 2:4], QL, bst, np_, 2)

            def cback(g, tl, hook=None):
                t = tiles_of(g)[tl]
                pz = 2 + (tl % 2)
                sl = tl % 2
                srow, bst = tstat[(g, tl)]
                kt = 0 if meta_pass else t + 1
                kc0 = 0 if meta_pass else NMETA + t * 128
                cs_ap = cs_meta[:, :] if meta_pass else cs_tok[:, t, :]
                sn_ap = sn_meta[:, :] if meta_pass else sn_tok[:, t, :]
                P.task("act", [lambda e: e.mul(cn[sl][0:np_, 0:256], pp[pz][0:np_, 0:256], srow[0:np_, 2:3]),
                               lambda e: e.mul(cn[sl][0:np_, 256:512], pp[pz][0:np_, 256:512], srow[0:np_, 3:4])],
                       reads=[ppb[pz][0], bst], writes=[Bcn[sl]])
                P.task("dve", [
                    lambda e: e.tensor_tensor(krt[0:np_, 32:64], pp[pz][0:np_, 512:544], cs_ap[0:np_, :], op=ALU.mult),
                    lambda e: e.tensor_tensor(krt[0:np_, 64:80], pp[pz][0:np_, 528:544], sn_ap[0:np_, 0:16], op=ALU.mult),
                    lambda e: e.tensor_tensor(krt[0:np_, 80:96], pp[pz][0:np_, 512:528], sn_ap[0:np_, 16:32], op=ALU.mult)],
                    reads=[ppb[pz][1], B_const], writes=[Bkrt])
                P.task("dve", lambda e: e.tensor_tensor(krt[0:np_, 0:32], krt[0:np_, 32:64], krt[0:np_, 64:96], op=ALU.add),
                       reads=[Bkrt], writes=[Bkrt])
                P.task("dve", lambda e: e.tensor_copy(
                    ktm[0:np_, :, DN:DN + DR], krt[0:np_, 0:32].unsqueeze(1).to_broadcast([np_, NH, DR])),
                    reads=[Bkrt], writes=[Bktm_r])
                P.task("pe", [(lambda e, c=c: e.matmul(
                    pp[0][:, c * 128:c * 128 + np_], lhsT=cn[sl][0:np_, c * 128:(c + 1) * 128], rhs=ident[0:np_, 0:np_],
                    start=True, stop=True)) for c in range(4)],
                    reads=[Bcn[sl], B_const], writes=[ppb[0][0]])
                fns = [lambda e: e.copy(ckvT[:, :, 0:np_], pp[0][:, 256:512].rearrange("p (c t) -> p c t", c=2)[:, :, 0:np_])]
                wr = [BckvT]
                if not meta_pass:
                    fns.append(lambda e: e.copy(cnTq[:, :, t * 128:(t + 1) * 128], pp[0][:, 0:256].rearrange("p (c t) -> p c t", c=2)))
                    wr.append(B_cnTq[t])
                P.task("act", fns, reads=[ppb[0][0]], writes=wr)
                if hook is not None:
                    hook()
                for half in range(2):
                    P.task("pe", [(lambda e, kc=kc, half=half: e.matmul(
                        pp[1][0:np_, half * 512:(half + 1) * 512], lhsT=ckvT[:, kc, 0:np_], rhs=wkv_sb[:, kc, half * 512:(half + 1) * 512],
                        start=(kc == 0), stop=(kc == 1))) for kc in range(2)],
                        reads=[BckvT, B_wkv], writes=[ppb[1][half]])
                kvv = pp[1][:].rearrange("p (h d) -> p h d", h=NH)
                P.task("act", [lambda e: e.copy(ktm[0:np_, :, 0:DN], kvv[0:np_, :, 0:DN]),
                               lambda e: e.copy(Vsb[0:np_, kt, 0:NH * (DV + 1)].rearrange("p (h d) -> p h d", h=NH)[:, :, 0:DV], kvv[0:np_, :, DN:DN + DV])],
                       reads=[ppb[1][0], ppb[1][1]], writes=[Bktm_n, B_V[kt]])
                P.task("pe", [(lambda e, h=h: e.matmul(
                    pp[0][0:96, h * 128:h * 128 + np_], lhsT=ktm[0:np_, h, :], rhs=ident[0:np_, 0:np_],
                    start=True, stop=True)) for h in range(NH)],
                    reads=[Bktm_n, Bktm_r, B_const], writes=[ppb[0][0], ppb[0][1]])
                P.task("act", lambda e: e.copy(
                    KT[0:96, :, kc0:kc0 + np_], pp[0][0:96, :].rearrange("p (h t) -> p h t", h=NH)[:, :, 0:np_]),
                    reads=[ppb[0][0], ppb[0][1]], writes=[B_KT[kt]])

            CA = A.get("C")
            conv_q = []

            def conv_drain(n=2):
                saved, P.after_dve = P.after_dve, None
                for _ in range(min(n, len(conv_q))):
                    k, c = conv_q.pop(0)
                    conv_emit(CA, k, c)
                P.after_dve = saved

            front(0)
            if A.get("load_win") is not None:
                A["load_win"]()
            mid(0)
            g_need = min(1, ngroups - 1)
            for g in range(ngroups):
                nt = len(tiles_of(g))
                if g + 1 < ngroups:
                    front(g + 1)
                valgate(g)
                if not meta_pass and g == g_need:
                    win_load(CA, 0)
                    conv_q.extend((k, c) for k in range(CK) for c in range(4))
                    P.after_dve = conv_drain
                cfront(g, 0)
                if nt > 1:
                    cfront(g, 1)
                for tl in range(nt):
                    cback(g, tl, (lambda tl=tl: cfront(g, tl + 2)) if tl + 2 < nt else None)
                if g + 1 < ngroups:
                    mid(g + 1)
            P.after_dve = None
            if meta_pass:
                P.dma("sp", DS["dupm"], [
                    lambda e: e.dma_start(out=upre_d[:, :, 15:15 + NMETA].rearrange("c p n -> p c n"), in_=upre_meta[:]),
                    lambda e: e.dma_start(out=upre_d[:, :, 0:15].rearrange("c p n -> p c n"), in_=zpad[:, :, 0:15]),
                    lambda e: e.dma_start(out=upre_d[:, :, 15 + L:UPW].rearrange("c p n -> p c n"), in_=zpad[:, :, 0:15])],
                    reads=[B_upm, B_ones], writes=[B_upd_pad])
            else:
                pending0[:] = conv_q

        def phaseB(seq, Bt):
            (w_out_sb, QT, qtm, qrt, pT, onT, rs, bcs, h2, hs, hfT, wgu, wdb, sg, aT) = (Bt[k] for k in
                ("w_out_sb", "QT", "qtm", "qrt", "pT", "onT", "rs", "bcs", "h2", "hs", "hfT", "wgu", "wdb", "sg", "aT"))
            CB = Bt["C"]
            CB["Bcsq"] = Bt["Bw"]["sg"]
            cq = []

            def cdrain(n):
                for _ in range(min(n, len(cq))):
                    k, c = cq.pop(0)
                    conv_emit(CB, k, c)
            Bw = Bt["Bw"]
            def q_front(b, tl, qi):
                t = 4 * b + tl
                sl = tl % 2
                P.task("pe", [(lambda e, kc=kc: e.matmul(
                    pp[qi][:, 0:480], lhsT=cnTq[:, kc, t * 128:(t + 1) * 128], rhs=wq_sb[:, kc, 0:480],
                    start=(kc == 0), stop=(kc == 1))) for kc in range(2)],
                    reads=[B_cnTq[t], B_wq], writes=[ppb[qi][0]])
                P.task("pe", [(lambda e, kc=kc: e.matmul(
                    pp[qi][:, 512:800], lhsT=cnTq[:, kc, t * 128:(t + 1) * 128], rhs=wq_sb[:, kc, 480:768],
                    start=(kc == 0), stop=(kc == 1))) for kc in range(2)],
                    reads=[B_cnTq[t], B_wq], writes=[ppb[qi][1]])
                qa = pp[qi][:, 0:480].rearrange("p (h d) -> p h d", h=5)
                qb = pp[qi][:, 512:800].rearrange("p (h d) -> p h d", h=3)
                P.task("act", [lambda e: e.copy(qtm[sl][:, 0:5, 0:DN], qa[:, :, 0:DN]),
                               lambda e: e.copy(qtm[sl][:, 5:8, 0:DN], qb[:, :, 0:DN])],
                       reads=[ppb[qi][0], ppb[qi][1]], writes=[Bw["qtm_n"][sl]])
                fns = []
                for (qv, h0, nh) in ((qa, 0, 5), (qb, 5, 3)):
                    csb = cs_tok[:, t, :].unsqueeze(1).to_broadcast([128, nh, 32])
                    snb = sn_tok[:, t, :].unsqueeze(1).to_broadcast([128, nh, 32])
                    fns.append(lambda e, qv=qv, h0=h0, nh=nh, csb=csb: e.tensor_tensor(
                        qrt[:, h0:h0 + nh, 0:32], qv[:, :, DN:DN + 32], csb, op=ALU.mult))
                    fns.append(lambda e, qv=qv, h0=h0, nh=nh, snb=snb: e.tensor_tensor(
                        qrt[:, h0:h0 + nh, 32:48], qv[:, :, DN + 16:DN + 32], snb[:, :, 0:16], op=ALU.mult))
                    fns.append(lambda e, qv=qv, h0=h0, nh=nh, snb=snb: e.tensor_tensor(
                        qrt[:, h0:h0 + nh, 48:64], qv[:, :, DN:DN + 16], snb[:, :, 16:32], op=ALU.mult))
                P.task("dve", fns, reads=[ppb[qi][0], ppb[qi][1], B_const], writes=[Bw["qrt"]])
                P.task("dve", lambda e: e.tensor_tensor(qtm[sl][:, :, DN:DN + DR], qrt[:, :, 0:32], qrt[:, :, 32:64], op=ALU.add),
                       reads=[Bw["qrt"]], writes=[Bw["qtm_r"][sl]])

            def q_back(b, tl, ti):
                sl = tl % 2
                P.task("pe", [(lambda e, h=h: e.matmul(
                    pp[ti][0:96, h * 128:(h + 1) * 128], lhsT=qtm[sl][:, h, :], rhs=ident[:], start=True, stop=True))
                    for h in range(NH)],
                    reads=[Bw["qtm_n"][sl], Bw["qtm_r"][sl], B_const], writes=[ppb[ti][0], ppb[ti][1]])
                P.task("act", lambda e: e.copy(
                    QT[0:96, :, tl * 128:(tl + 1) * 128], pp[ti][0:96, :].rearrange("p (h t) -> p h t", h=NH)),
                    reads=[ppb[ti][0], ppb[ti][1]], writes=[Bw["QT"][tl]])

            def qstage(b):
                for tl in range(4):
                    q_front(b, tl, 0)
                    q_back(b, tl, 1)

            P.task("pool", lambda e: e.memset(QT[96:128, :, :], 0.0), writes=Bw["QT"])
            for b in range(NB):
                P.dma("sp", Bt["dwout"], lambda e: e.dma_start(out=w_out_sb.rearrange("p k n -> p (k n)"), in_=wout_s),
                      reads=[B_wout_s], writes=[Bw["w_out"]] + Bw["aT"])
                if b == 0:
                    for (k_, c_) in pending0:
                        conv_emit(CB, k_, c_)
                    pending0[:] = []
                    ln_emit(CB, uTb[0], B_uTb[0], 2)
                    qstage(0)
                if b + 1 < NB:
                    win_load(CB, b + 1)
                    cq.extend((k, c) for k in range(CK) for c in range(4))
                groups = [[0]] + [[1 + 2 * i, 2 + 2 * i] for i in range(NT // 2)]

                def kinfo(kt):
                    if kt == 0:
                        return NMETA, 0
                    return 128, NMETA + (kt - 1) * 128

                items = [(h, gi) for h in range(NH) for gi in range(len(groups))]

                def emit_qk(idx):
                    h, gi = items[idx]
                    grp = groups[gi]
                    sp_i = 1 + (idx % 3)
                    psl = idx % 4
                    fns = []
                    for ii, kt in enumerate(grp):
                        nk, k0 = kinfo(kt)
                        fns.append(lambda e, ii=ii, nk=nk, k0=k0: e.matmul(
                            pp[sp_i][0:nk, ii * 512:(ii + 1) * 512], lhsT=KT[:, h, k0:k0 + nk], rhs=QT[:, h, :],
                            start=True, stop=True))
                    P.task("pe", fns, reads=[B_KT[kt] for kt in grp] + Bw["QT"],
                           writes=[ppb[sp_i][ii] for ii in range(len(grp))])
                    nkg = kinfo(grp[0])[0]
                    wcols = 512 * len(grp)
                    P.task("act", lambda e: e.activation(
                        out=pT[psl][0:nkg, 0:wcols], in_=pp[sp_i][0:nkg, 0:wcols], func=AF.Exp, scale=ATTN_SCALE),
                        reads=[ppb[sp_i][ii] for ii in range(len(grp))], writes=[Bw["pT"][psl]])

                def emit_pv(idx):
                    h, gi = items[idx]
                    grp = groups[gi]
                    psl = idx % 4
                    ob = ppb[0][h % 2]
                    o_ps = pp[0][:, (h % 2) * 512:(h % 2) * 512 + 512]
                    first = (gi == 0)
                    last_g = (gi == len(groups) - 1)
                    fns2 = []
                    for ii, kt in enumerate(grp):
                        nk, _ = kinfo(kt)
                        fns2.append(lambda e, ii=ii, kt=kt, nk=nk: e.matmul(
                            o_ps[:, :], lhsT=Vsb[0:nk, kt, h * (DV + 1):h * (DV + 1) + 128], rhs=pT[psl][0:nk, ii * 512:(ii + 1) * 512],
                            start=(first and ii == 0), stop=(last_g and ii == len(grp) - 1)))
                    P.task("pe", fns2, reads=[B_V[kt] for kt in grp] + [Bw["pT"][psl]], writes=[ob])
                    if not last_g:
                        return
                    P.task("dve", lambda e: e.reciprocal(rs[DV:DV + 1, :], o_ps[DV:DV + 1, :]),
                           reads=[ob], writes=[Bw["rs"]])

                    def tail():
                        P.task("pe", lambda e: e.matmul(o_ps[DV:2 * DV, :], lhsT=ones32[DV:DV + 1, 0:DV], rhs=rs[DV:DV + 1, :],
                                                        start=True, stop=True),
                               reads=[Bw["rs"], B_ones], writes=[ob])
                        P.task("act", lambda e: e.copy(bcs[0:DV, :], o_ps[DV:2 * DV, :]), reads=[ob], writes=[Bw["bcs"]])
                        po = (h % 2) * DV
                        P.task("dve", lambda e: e.tensor_tensor(
                            onT[po:po + DV, h // 2, :], o_ps[0:DV, :], bcs[0:DV, :], op=ALU.mult),
                            reads=[ob, Bw["bcs"]], writes=[Bw["onT"][h]])
                    return tail

                emit_qk(0)
                emit_qk(1)
                tails = []
                for idx in range(len(items)):
                    if idx + 2 < len(items):
                        emit_qk(idx + 2)
                    t_ = emit_pv(idx)
                    if t_ is not None:
                        tails.append((idx + 3, t_))
                    while tails and tails[0][0] <= idx:
                        tails.pop(0)[1]()
                    if idx >= 18:
                        cdrain(1)
                for _, t_ in tails:
                    t_()
                srow, bst = next_stat()
                for tl in range(4):
                    t = 4 * b + tl
                    sl = tl % 2
                    pi = 2 + sl
                    P.dma("sp", Bt["dx"][sl], lambda e, tl=tl, t=t: e.dma_start(out=h2[:, tl, :], in_=x[seq, t * 128:(t + 1) * 128, :]),
                          writes=[Bw["h2"][tl]])
                    for half in range(2):
                        fns = []
                        for c in range(4):
                            fns.append(lambda e, c=c, half=half, t=t, pi=pi: e.matmul(
                                pp[pi][:, half * 512:(half + 1) * 512], lhsT=uTb[b % 2][:, c, tl * 128:(tl + 1) * 128],
                                rhs=w_out_sb[:, c, half * 512:(half + 1) * 512], start=(c == 0), stop=False))
                        P.task("pe", fns, reads=[B_uTb[b % 2], Bw["w_out"]], writes=[ppb[pi][half]])
                    for half in range(2):
                        fns = []
                        for j in range(4):
                            fns.append(lambda e, j=j, half=half, tl=tl, pi=pi: e.matmul(
                                pp[pi][:, half * 512:(half + 1) * 512], lhsT=onT[:, j, tl * 128:(tl + 1) * 128],
                                rhs=w_out_sb[:, 4 + j, half * 512:(half + 1) * 512], start=False, stop=(j == 3)))
                        P.task("pe", fns, reads=[Bw["w_out"]] + Bw["onT"], writes=[ppb[pi][half]])
                    P.task("dve", lambda e, tl=tl, pi=pi: e.tensor_tensor(h2[:, tl, :], pp[pi][:], h2[:, tl, :], op=ALU.add),
                           reads=[ppb[pi][0], ppb[pi][1]], writes=[Bw["h2"][tl]])
                    P.task("act", lambda e, tl=tl, sl=sl, srow=srow: e.activation(
                        out=hs[sl][:], in_=h2[:, tl, :], func=AF.Square, accum_out=srow[:, tl:tl + 1]),
                        reads=[Bw["h2"][tl]], writes=[Bw["hs"][sl], bst])
                rstd_from_ssq(srow[:, 0:4], srow[:, 4:8], D, bst, 128, 4)
                for tl in range(4):
                    sl = tl % 2
                    P.task("dve", lambda e, tl=tl, sl=sl, srow=srow: e.tensor_scalar(
                        hs[sl][:], h2[:, tl, :], srow[:, 4 + tl:5 + tl], None, op0=ALU.mult),
                        reads=[Bw["h2"][tl], bst], writes=[Bw["hs"][sl]])
                    pj = sl
                    P.task("pe", [(lambda e, c=c, sl=sl, pj=pj: e.matmul(
                        pp[pj][:, c * 128:(c + 1) * 128], lhsT=hs[sl][:, c * 128:(c + 1) * 128], rhs=ident[:],
                        start=True, stop=True)) for c in range(8)],
                        reads=[Bw["hs"][sl], B_const], writes=[ppb[pj][0], ppb[pj][1]])
                    P.task("act", lambda e, tl=tl, pj=pj: e.copy(
                        hfT[:, :, tl * 128:(tl + 1) * 128], pp[pj][:].rearrange("p (c t) -> p c t", c=8)),
                        reads=[ppb[pj][0], ppb[pj][1]], writes=[Bw["hfT"][tl]])
                for j in range(NJ):
                    ws = j % 2
                    P.dma("sp", Bt["dwgu"][ws], [lambda e, j=j, ws=ws: e.dma_start(out=wgu[ws][:, 0, :], in_=wg_s[j]),
                                                   lambda e, j=j, ws=ws: e.dma_start(out=wgu[ws][:, 1, :], in_=wu_s[j])],
                          reads=B_wg_parts + B_wu_parts, writes=[Bw["wgu"][ws]])
                    pi = j % 2
                    for gu in range(2):
                        P.task("pe", [(lambda e, k=k, gu=gu, ws=ws, pi=pi: e.matmul(
                            pp[pi][:, gu * 512:(gu + 1) * 512], lhsT=wgu[ws][:, gu, k * 128:(k + 1) * 128], rhs=hfT[:, k, :],
                            start=(k == 0), stop=(k == 7))) for k in range(8)],
                            reads=[Bw["wgu"][ws]] + Bw["hfT"], writes=[ppb[pi][gu]])
                    sl = j % 2
                    P.task("act", lambda e, sl=sl, pi=pi: e.activation(out=sg[sl][:], in_=pp[pi][:, 0:512], func=AF.Silu),
                           reads=[ppb[pi][0]], writes=[Bw["sg"][sl]])
                    P.task("dve", lambda e, sl=sl, pi=pi, j=j: e.tensor_tensor(aT[:, j, :], pp[pi][:, 512:1024], sg[sl][:], op=ALU.mult),
                           reads=[ppb[pi][1], Bw["sg"][sl]], writes=[Bw["aT"][j], Bw["w_out"]])
                    cdrain(4)
                    if b + 1 < NB:
                        if j in (1, 6, 11, 16):
                            q_front(b + 1, (j - 1) // 5, 2)
                        if j in (4, 9, 14, 19):
                            q_back(b + 1, (j - 4) // 5, 3)
                cdrain(len(cq))
                if b + 1 < NB:
                    ln_emit(CB, uTb[(b + 1) % 2], B_uTb[(b + 1) % 2], 2)
                for j in range(NJ):
                    ws = j % 3
                    P.dma("sp", Bt["dwd"][ws], lambda e, j=j, ws=ws: e.dma_start(out=wdb[ws][:], in_=wd_s[j]),
                          reads=[B_wd_s], writes=[Bw["wd"][ws]])
                    fns = []
                    for tl in range(4):
                        for half in range(2):
                            fns.append(lambda e, tl=tl, half=half, j=j, ws=ws: e.matmul(
                                pp[tl][:, half * 512:(half + 1) * 512], lhsT=aT[:, j, tl * 128:(tl + 1) * 128],
                                rhs=wdb[ws][:, half * 512:(half + 1) * 512], start=(j == 0), stop=(j == NJ - 1)))
                    P.task("pe", fns, reads=[Bw["aT"][j], Bw["wd"][ws]], writes=[ppb[i][hh] for i in range(4) for hh in range(2)])
                srow, bst = next_stat()
                for tl in range(4):
                    sl = tl % 2
                    P.task("dve", lambda e, tl=tl: e.tensor_tensor(h2[:, tl, :], pp[tl][:], h2[:, tl, :], op=ALU.add),
                           reads=[ppb[tl][0], ppb[tl][1]], writes=[Bw["h2"][tl]])
                    P.task("act", lambda e, sl=sl, tl=tl, srow=srow: e.activation(
                        out=hs[sl][:], in_=h2[:, tl, :], func=AF.Square, accum_out=srow[:, tl:tl + 1]),
                        reads=[Bw["h2"][tl]], writes=[Bw["hs"][sl], bst])
                rstd_from_ssq(srow[:, 0:4], srow[:, 4:8], D, bst, 128, 4)
                for tl in range(4):
                    t = 4 * b + tl
                    sl = tl % 2
                    P.task("dve", lambda e, tl=tl, srow=srow: e.scalar_tensor_tensor(
                        h2[:, tl, :], h2[:, tl, :], srow[:, 4 + tl:5 + tl], gfin[:], op0=ALU.mult, op1=ALU.mult),
                        reads=[bst, B_const], writes=[Bw["h2"][tl]])
                    P.dma("sp", Bt["dout"][sl], lambda e, tl=tl, t=t: e.dma_start(out=out[seq, t * 128:(t + 1) * 128, :], in_=h2[:, tl, :]),
                          reads=[Bw["h2"][tl]])

        def allocA(stack, meta_pass):
            A = {}
            A["w_in_sb"] = sbuf(stack, "w_in_sb", [128, 8, INW], BF16)
            A["xt"] = [sbuf(stack, "xtA%d" % i, [128, D], F32) for i in range(4)]
            A["xs"] = [sbuf(stack, "xsA%d" % i, [128, D], BF16) for i in range(2)]
            A["hnT"] = sbuf(stack, "hnT", [128, 8, 512], BF16)
            A["th"] = [sbuf(stack, "th%d" % i, [128, 512], F32) for i in range(2)]
            A["ust"] = [sbuf(stack, "ust%d" % i, [128, 512], F32) for i in range(2)]
            A["Bust"] = [Buf(), Buf()]
            A["dust"] = DS["dust"]
            A["cn"] = [sbuf(stack, "cn%d" % i, [128, 512], BF16) for i in range(2)]
            A["ckvT"] = sbuf(stack, "ckvT", [128, 2, 128], BF16)
            A["ktm"] = sbuf(stack, "ktm", [128, NH, DN + DR], BF16)
            A["krt"] = sbuf(stack, "krt", [128, 96], F32)
            A["Bx"] = [Buf() for _ in range(4)]
            A["Bxs"] = [Buf(), Buf()]
            A["BhnT"] = [Buf() for _ in range(4)]
            A["Bth"] = [Buf(), Buf()]
            A["Bup"] = [[Buf() for _ in range(4)] for _ in range(NB)]
            A["Bup_pad"] = Buf()
            A["Bcn"] = [Buf(), Buf()]
            A["BckvT"] = Buf()
            A["Bktm_n"] = Buf()
            A["Bktm_r"] = Buf()
            A["Bkrt"] = Buf()
            A["Bwin"] = Buf()
            A["dx"] = DS["dxA"]
            A["dwin"] = DS["dwin"]
            if not meta_pass:
                A["C"] = C_all
                C_all["csq"], C_all["Bcsq"] = A["th"], A["Bth"]
            return A

        def allocB(stack):
            Bt = {}
            Bt["QT"] = sbuf(stack, "QT", [128, NH, 512], BF16)
            Bt["qtm"] = [sbuf(stack, "qtm%d" % i, [128, NH, DN + DR], BF16) for i in range(2)]
            Bt["qrt"] = sbuf(stack, "qrt", [128, NH, 64], F32)
            Bt["pT"] = [sbuf(stack, "pT%d" % i, [128, 1024], BF16) for i in range(4)]
            Bt["onT"] = sbuf(stack, "onT", [128, 4, 512], BF16)
            Bt["bcs"] = sbuf(stack, "bcs", [128, 512], F32)
            Bt["rs"] = Bt["bcs"]
            Bt["h2"] = sbuf(stack, "h2", [128, 4, D], F32)
            Bt["hs"] = [sbuf(stack, "hs%d" % i, [128, D], BF16) for i in range(2)]
            Bt["hfT"] = sbuf(stack, "hfT", [128, 8, 512], BF16)
            Bt["wgu"] = [sbuf(stack, "wgu%d" % i, [128, 2, 1024], BF16) for i in range(2)]
            Bt["wdb"] = [sbuf(stack, "wdb%d" % i, [128, D], BF16) for i in range(3)]
            Bt["C"] = None
            Bt["sg"] = [sbuf(stack, "sg%d" % i, [128, 512], F32) for i in range(2)]
            Bt["aT"] = sbuf(stack, "aT", [128, NJ, 512], BF16)
            Bt["w_out_sb"] = Bt["aT"][:, 0:16, :].rearrange("p a b -> p (a b)").rearrange("p (k n) -> p k n", k=8)
            Bt["Bw"] = {
                "w_out": Buf(), "QT": [Buf() for _ in range(4)], "qtm_n": [Buf(), Buf()], "qtm_r": [Buf(), Buf()],
                "qrt": Buf(), "pT": [Buf() for _ in range(4)], "onT": [Buf() for _ in range(NH)], "rs": Buf(), "bcs": Buf(),
                "xt": [Buf(), Buf()], "h2": [Buf() for _ in range(4)], "hs": [Buf(), Buf()], "hfT": [Buf() for _ in range(4)],
                "wgu": [Buf() for _ in range(3)], "wd": [Buf() for _ in range(3)], "sg": [Buf(), Buf()],
                "aT": [Buf() for _ in range(NJ)], "yt": [Buf(), Buf()], "ot": [Buf(), Buf()],
            }
            Bt["C"] = C_all
            C_all["csq"] = Bt["sg"]
            Bt["dx"] = DS["dxB"]
            Bt["dwgu"] = DS["dwgu"]
            Bt["dwd"] = DS["dwd"]
            Bt["dout"] = DS["dout"]
            Bt["dwout"] = DS["dwout"]
            return Bt

        DS = {
            "dxA": [P.new_dma_sem("dxA%d" % i) for i in range(4)], "dwin": P.new_dma_sem("dwin"),
            "dxB": [P.new_dma_sem("dxB0"), P.new_dma_sem("dxB1")],
            "dwgu": [P.new_dma_sem("dwgu%d" % i) for i in range(3)],
            "dwd": [P.new_dma_sem("dwd%d" % i) for i in range(3)],
            "dout": [P.new_dma_sem("dout0"), P.new_dma_sem("dout1")], "dwout": P.new_dma_sem("dwout"),
            "dust": [P.new_dma_sem("dust0"), P.new_dma_sem("dust1")], "duwin": P.new_dma_sem("duwin"), "dupm": P.new_dma_sem("dupm"),
        }
        C_all = allocC(st, "P", csq=[None, None])
        pending0 = []
        with ExitStack() as s1:
            A = allocA(s1, True)
            P.dma("sp", A["dwin"], lambda e: e.dma_start(out=A["w_in_sb"][:].rearrange("p k n -> p (k n)"), in_=win_s),
                  reads=B_win_parts, writes=[A["Bwin"]])
            if stop >= 2 and stop != 3.5:
                phaseA(0, A, True)
            P.barrier()

        for seq in range(NSEQ):
            with ExitStack() as s1:
                A = allocA(s1, False)
                A["load_win"] = lambda A=A: P.dma("sp", A["dwin"], lambda e: e.dma_start(
                    out=A["w_in_sb"][:].rearrange("p k n -> p (k n)"), in_=win_s), reads=B_win_parts, writes=[A["Bwin"]])
                if stop >= 3:
                    phaseA(seq, A, False)
                P.barrier()
            with ExitStack() as s1:
                Bt = allocB(s1)
                if stop >= 5:
                    phaseB(seq, Bt)
                P.barrier()
        P.barrier()
    return nc


def _rope_tables(S):
    inv = (1.0 / (np.float32(10000.0) ** (np.arange(0, DR, 2, dtype=np.float32) / np.float32(DR)))).astype(np.float32)
    pos = np.arange(S + NMETA, dtype=np.float32)
    ang = (pos[:, None] * inv[None, :]).astype(np.float32)
    cos = np.cos(ang).astype(np.float32)
    sin = np.sin(ang).astype(np.float32)
    cs = np.concatenate([cos, cos], axis=1)
    sn = np.concatenate([-sin, sin], axis=1)
    NT = S // 128
    cs_tok = np.ascontiguousarray(cs[NMETA:].reshape(NT, 128, 32).transpose(1, 0, 2))
    sn_tok = np.ascontiguousarray(sn[NMETA:].reshape(NT, 128, 32).transpose(1, 0, 2))
    return cs_tok, sn_tok, np.ascontiguousarray(cs[:NMETA]), np.ascontiguousarray(sn[:NMETA])


def _layout_weights(meta_tokens, attn_norm_g, w_in, q_norm_g, w_q_up, kv_norm_g, w_kv_up, conv_dw_w, conv_dw_b,
                    conv_ln_g, conv_ln_b, w_out, ffn_norm_g, w_gate, w_up, w_down, final_norm_g, S):
    f = lambda a: np.ascontiguousarray(np.asarray(a, dtype=np.float32))
    cs_tok, sn_tok, cs_meta, sn_meta = _rope_tables(S)
    vec = lambda v, k: f(np.asarray(v).reshape(k, 128).T)
    return {
        "meta": f(meta_tokens),
        "w_in_l": f(np.asarray(w_in[0]).reshape(8, 128, INW).transpose(1, 0, 2)),
        "wq_l": f(np.asarray(w_q_up[0]).reshape(2, 128, 768).transpose(1, 0, 2)),
        "wkv_l": f(np.asarray(w_kv_up[0]).reshape(2, 128, 1024).transpose(1, 0, 2)),
        "w_out_l": f(np.asarray(w_out[0]).reshape(8, 128, D).transpose(1, 0, 2)),
        "wg_l": f(np.asarray(w_gate[0]).reshape(8, 128, NJ, 128).transpose(2, 1, 0, 3)),
        "wu_l": f(np.asarray(w_up[0]).reshape(8, 128, NJ, 128).transpose(2, 1, 0, 3)),
        "wd_l": f(np.asarray(w_down[0]).reshape(NJ, 128, D)),
        "gattn_l": vec(attn_norm_g[0], 8),
        "gq_l": vec(q_norm_g[0], 2),
        "gkv_l": vec(kv_norm_g[0], 2),
        "gffn_l": vec(ffn_norm_g[0], 8),
        "cw_l": f(np.asarray(conv_dw_w[0]).T.reshape(4, 128, CK).transpose(1, 0, 2)),
        "cb_l": vec(conv_dw_b[0], 4),
        "lng_l": vec(conv_ln_g[0], 4),
        "lnb_l": vec(conv_ln_b[0], 4),
        "gfin_l": f(np.broadcast_to(np.asarray(final_norm_g)[None, :], (128, D))),
        "ident_l": np.eye(128, dtype=np.float32).astype(ml_dtypes.bfloat16),
        "cs_tok_l": cs_tok, "sn_tok_l": sn_tok, "cs_meta_l": cs_meta, "sn_meta_l": sn_meta,
    }


_NC_CACHE = {}


def kernel(x_prompt, x_sample, meta_tokens, attn_norm_g, w_in, q_norm_g, w_q_up, kv_norm_g, w_kv_up,
           conv_dw_w, conv_dw_b, conv_ln_g, conv_ln_b, w_out, ffn_norm_g, w_gate, w_up, w_down, final_norm_g):
    x_prompt = np.asarray(x_prompt, dtype=np.float32)
    x_sample = np.asarray(x_sample, dtype=np.float32)
    nb_p, S, _ = x_prompt.shape
    nb_s = x_sample.shape[0]
    assert x_sample.shape[1] == S
    ntot = nb_p + nb_s
    assert ntot % N_CORES == 0
    nseq = ntot // N_CORES
    key = (nseq, S)
    if key not in _NC_CACHE:
        _NC_CACHE[key] = build(nseq, S)
    nc = _NC_CACHE[key]
    wl = _layout_weights(meta_tokens, attn_norm_g, w_in, q_norm_g, w_q_up, kv_norm_g, w_kv_up, conv_dw_w, conv_dw_b,
                         conv_ln_g, conv_ln_b, w_out, ffn_norm_g, w_gate, w_up, w_down, final_norm_g, S)
    seqs = [x_prompt[i] for i in range(nb_p)] + [x_sample[i] for i in range(nb_s)]
    in_maps = []
    for c in range(N_CORES):
        m = dict(wl)
        m["x"] = np.ascontiguousarray(np.stack(seqs[c * nseq:(c + 1) * nseq], axis=0))
        in_maps.append(m)
    res = run_bass_kernel_spmd(nc, in_maps, core_ids=list(range(N_CORES)))
    outs = [res.results[c]["out"] for c in range(N_CORES)]
    allo = np.concatenate(outs, axis=0)
    return (np.ascontiguousarray(allo[:nb_p]), np.ascontiguousarray(allo[nb_p:]))
```

```python
import math
from contextlib import ExitStack

import numpy as np
import ml_dtypes
import concourse.bass as bass
import concourse.mybir as mybir
from concourse.bass_utils import run_bass_kernel_spmd

F32 = mybir.dt.float32
BF16 = mybir.dt.bfloat16
AF = mybir.ActivationFunctionType
ALU = mybir.AluOpType

D = 1024
NMETA = 16
CCH = 512
CK = 31
NH = 8
DN = 64
DR = 32
DV = 64
QL = 256
DFF = 2816
NJ = DFF // 128
INW = 1568
EPS = 1e-6
ATTN_SCALE = 1.0 / math.sqrt(DN + DR)
N_CORES = 8


class Buf:
    __slots__ = ("name", "writers", "readers", "excl")

    def __init__(self, name="", excl=False):
        self.name = name
        self.excl = excl
        self.writers = {}
        self.readers = {}


def _merge(d, tok):
    k = tok[2]
    if k not in d or d[k][1] < tok[1]:
        d[k] = tok


class Prog:
    ENGS = ("pe", "act", "dve", "pool", "sp")

    def __init__(self, nc, stack):
        self.nc = nc
        self.stack = stack
        self.eng = {"pe": nc.tensor, "act": nc.scalar, "dve": nc.vector, "pool": nc.gpsimd, "sp": nc.sync}
        self.sem = {e: stack.enter_context(nc.semaphore("s_" + e)) for e in self.ENGS}
        self.cnt = {e: 0 for e in self.ENGS}
        self.waited = {e: {} for e in self.ENGS}
        self.dma_sems = []
        self.after_dve = None

    def _waits_for(self, eng, toks):
        best = {}
        for t in toks:
            sem, val, key, src = t
            if src == "pe" and eng == "pe":
                continue
            if key not in best or best[key][1] < val:
                best[key] = (sem, val)
        w = self.waited[eng]
        out = []
        for key, (sem, val) in best.items():
            if w.get(key, -1) >= val:
                continue
            w[key] = val
            out.append((sem, val))
        return out

    def _deps(self, reads, writes, extra):
        toks = list(extra)
        for b in reads:
            toks += list(b.writers.values())
            if b.excl:
                toks += list(b.readers.values())
        for b in writes:
            toks += list(b.writers.values())
            toks += list(b.readers.values())
        return toks

    def _emit(self, eng, waits, fns, inc, each):
        e = self.eng[eng]
        for sem, val in waits:
            e.wait_ge(sem, val)
        n = len(fns)
        for i, f in enumerate(fns):
            ins = f(e)
            if each or i == n - 1:
                ins.then_inc(inc[0], inc[1])

    def _record(self, tok, reads, writes):
        for b in reads:
            _merge(b.readers, tok)
        for b in writes:
            b.writers = {tok[2]: tok}
            b.readers = {}

    def task(self, eng, fns, reads=(), writes=(), extra=()):
        if callable(fns):
            fns = [fns]
        waits = self._waits_for(eng, self._deps(reads, writes, extra))
        self.cnt[eng] += 1
        tok = (self.sem[eng], self.cnt[eng], eng, eng)
        self._emit(eng, waits, fns, (self.sem[eng], 1), False)
        self._record(tok, reads, writes)
        hook = self.after_dve
        if eng == "dve" and hook is not None:
            self.after_dve = None
            hook()
            self.after_dve = hook
        return tok

    def new_dma_sem(self, name):
        s = self.stack.enter_context(self.nc.semaphore(name))
        st = {"sem": s, "val": 0, "key": "dma%d_%s" % (len(self.dma_sems), name)}
        self.dma_sems.append(st)
        return st

    def dma(self, eng, dsem, fns, reads=(), writes=(), extra=()):
        if callable(fns):
            fns = [fns]
        waits = self._waits_for(eng, self._deps(reads, writes, extra))
        dsem["val"] += 16 * len(fns)
        tok = (dsem["sem"], dsem["val"], dsem["key"], None)
        self._emit(eng, waits, fns, (dsem["sem"], 16), True)
        self._record(tok, reads, writes)
        return tok

    def wait_all(self, eng, toks):
        e = self.eng[eng]
        for sem, val in self._waits_for(eng, toks):
            e.wait_ge(sem, val)

    def barrier(self):
        toks = [(self.sem[e], self.cnt[e], e, e) for e in self.ENGS if self.cnt[e] > 0]
        toks += [(d["sem"], d["val"], d["key"], None) for d in self.dma_sems if d["val"] > 0]
        for e in self.ENGS:
            self.wait_all(e, [t for t in toks if t[3] != e])


def build(NSEQ, S, stop=99):
    assert S % 512 == 0
    NT = S // 128
    NB = S // 512
    L = S + NMETA
    NKT = NT + 1
    UPW = S + NMETA + 30

    nc = bass.Bass("TRN2", target_bir_lowering=False)

    def din(name, shape, dt=F32):
        return nc.dram_tensor(name, list(shape), dt, kind="ExternalInput").ap()

    x = din("x", [NSEQ, S, D])
    meta = din("meta", [NMETA, D])
    w_in_l = din("w_in_l", [128, 8, INW])
    wq_l = din("wq_l", [128, 2, 768])
    wkv_l = din("wkv_l", [128, 2, 1024])
    w_out_l = din("w_out_l", [128, 8, D])
    wg_l = din("wg_l", [NJ, 128, 8, 128])
    wu_l = din("wu_l", [NJ, 128, 8, 128])
    wd_l = din("wd_l", [NJ, 128, D])
    gattn_l = din("gattn_l", [128, 8])
    gq_l = din("gq_l", [128, 2])
    gkv_l = din("gkv_l", [128, 2])
    gffn_l = din("gffn_l", [128, 8])
    cw_l = din("cw_l", [128, 4, CK])
    cb_l = din("cb_l", [128, 4])
    lng_l = din("lng_l", [128, 4])
    lnb_l = din("lnb_l", [128, 4])
    gfin_l = din("gfin_l", [128, D])
    ident_l = din("ident_l", [128, 128], BF16)
    cs_tok_l = din("cs_tok_l", [128, NT, 32])
    sn_tok_l = din("sn_tok_l", [128, NT, 32])
    cs_meta_l = din("cs_meta_l", [NMETA, 32])
    sn_meta_l = din("sn_meta_l", [NMETA, 32])
    out = nc.dram_tensor("out", [NSEQ, S, D], F32, kind="ExternalOutput").ap()

    win_s = nc.dram_tensor("win_s", [128, 8 * INW], BF16).ap()
    wout_s = nc.dram_tensor("wout_s", [128, 8 * D], BF16).ap()
    wg_s = nc.dram_tensor("wg_s", [NJ, 128, 1024], BF16).ap()
    wu_s = nc.dram_tensor("wu_s", [NJ, 128, 1024], BF16).ap()
    wd_s = nc.dram_tensor("wd_s", [NJ, 128, D], BF16).ap()
    upre_d = nc.dram_tensor("upre_d", [4, 128, UPW], F32).ap()

    with ExitStack() as st:
        P = Prog(nc, st)

        uid = [0]

        def sbuf(stack, name, shape, dt):
            uid[0] += 1
            return stack.enter_context(nc.sbuf_tensor("%s_%d" % (name, uid[0]), list(shape), dt))

        pp = [st.enter_context(nc.psum_tensor("pp%d" % i, [128, 1024], F32)) for i in range(4)]
        ppb = [[Buf("pp%d_%d" % (i, h), True) for h in range(2)] for i in range(4)]

        ident = sbuf(st, "ident", [128, 128], BF16)
        ones32 = sbuf(st, "ones32", [128, 128], F32)
        neghalf = sbuf(st, "neghalf", [128, 8], F32)
        zpad = sbuf(st, "zpad", [128, 4, 16], F32)
        gattn = sbuf(st, "gattn", [128, 8], F32)
        gq = sbuf(st, "gq", [128, 2], F32)
        gkv = sbuf(st, "gkv", [128, 2], F32)
        gffn = sbuf(st, "gffn", [128, 8], F32)
        cw = sbuf(st, "cw", [128, 4, CK], F32)
        cb = sbuf(st, "cb", [128, 4], F32)
        lng = sbuf(st, "lng", [128, 4], F32)
        lnb = sbuf(st, "lnb", [128, 4], F32)
        gfin = sbuf(st, "gfin", [128, D], F32)
        cs_tok = sbuf(st, "cs_tok", [128, NT, 32], F32)
        sn_tok = sbuf(st, "sn_tok", [128, NT, 32], F32)
        cs_meta = sbuf(st, "cs_meta", [NMETA, 32], F32)
        sn_meta = sbuf(st, "sn_meta", [NMETA, 32], F32)
        wq_sb = sbuf(st, "wq_sb", [128, 2, 768], BF16)
        wkv_sb = sbuf(st, "wkv_sb", [128, 2, 1024], BF16)
        KT = sbuf(st, "KT", [128, NH, L], BF16)
        Vsb = sbuf(st, "Vsb", [128, NKT, NH * (DV + 1) + 64], BF16)
        uTb = [sbuf(st, "uTb%d" % i, [128, 4, 512], BF16) for i in range(2)]
        cnTq = sbuf(st, "cnTq", [128, 2, S], BF16)
        upre_meta = sbuf(st, "upre_meta", [128, 4, NMETA], F32)
        stats = sbuf(st, "stats", [128, 8, 8], F32)

        B_const = Buf("const")
        B_wq = Buf("wq")
        B_wkv = Buf("wkv")
        B_KT = [Buf("KT%d" % i) for i in range(NKT)]
        B_V = [Buf("V%d" % i) for i in range(NKT)]
        B_uTb = [Buf("uTb0"), Buf("uTb1")]
        B_upd = [[Buf("upd%d_%d" % (g, c)) for c in range(4)] for g in range(NB)]
        B_upd_pad = Buf("upd_pad")
        B_cnTq = [Buf("cnTq%d" % i) for i in range(NT)]
        B_upm = Buf("upre_meta")
        B_stats = [Buf("stats%d" % i) for i in range(8)]
        B_win_s = Buf("win_s")
        B_win_parts = [Buf("win_s0"), Buf("win_s1")]
        B_wg_parts = [Buf("wg_s%d" % i) for i in range(4)]
        B_wu_parts = [Buf("wu_s%d" % i) for i in range(4)]
        B_wout_s = Buf("wout_s")
        B_wg_s = Buf("wg_s")
        B_wu_s = Buf("wu_s")
        B_wd_s = Buf("wd_s")
        stat_rr = [0]

        def next_stat():
            i = stat_rr[0] % 8
            stat_rr[0] += 1
            return stats[:, i, :], B_stats[i]

        dconst = P.new_dma_sem("dconst")
        dwq = P.new_dma_sem("dwq")
        dwkv = P.new_dma_sem("dwkv")

        consts = [(ident, ident_l), (gattn, gattn_l), (gq, gq_l), (gkv, gkv_l), (gffn, gffn_l), (cw, cw_l),
                  (cb, cb_l), (lng, lng_l), (lnb, lnb_l), (gfin, gfin_l), (cs_tok, cs_tok_l), (sn_tok, sn_tok_l),
                  (cs_meta, cs_meta_l), (sn_meta, sn_meta_l)]
        P.dma("sp", dconst, [(lambda e, d=d, s=s: e.dma_start(out=d[:], in_=s)) for d, s in consts], writes=[B_const])
        B_ones = Buf("ones")
        P.task("pool", [lambda e: e.memset(ones32[:], 1.0), lambda e: e.memset(neghalf[:], -0.5), lambda e: e.memset(zpad[:], 0.0),
                        lambda e: e.memset(Vsb[:], 0.0),
                        lambda e: e.memset(KT[96:128, :, :], 0.0)], writes=[B_ones] + B_V + B_KT)
        P.task("pool", lambda e: e.memset(
            Vsb[:, :, 0:NH * (DV + 1)].rearrange("p k (h d) -> p k h d", h=NH)[:, :, :, DV:DV + 1], 1.0), writes=B_V)

        def rstd_from_ssq(ssq_ap, out_ap, n, bst, np_=128, w=1):
            P.task("dve", lambda e: e.tensor_scalar(out_ap, ssq_ap, 1.0 / n, EPS, op0=ALU.mult, op1=ALU.add),
                   reads=[bst], writes=[bst])
            P.task("pool", lambda e: e.tensor_tensor(out_ap, out_ap, neghalf[0:np_, 0:w], op=ALU.pow),
                   reads=[bst, B_ones], writes=[bst])

        with ExitStack() as ps_:
            HW_ = 4 * INW
            st32 = [sbuf(ps_, "st32_%d" % i, [128, HW_], F32) for i in range(2)]
            stb = [sbuf(ps_, "stb_%d" % i, [128, HW_], BF16) for i in range(2)]
            B32 = [Buf("st32a"), Buf("st32b")]
            Bb = [Buf("stba"), Buf("stbb")]
            dst32 = [P.new_dma_sem("dst32a"), P.new_dma_sem("dst32b")]
            dstb = [P.new_dma_sem("dstba"), P.new_dma_sem("dstbb")]
            dcast = P.new_dma_sem("dcast")
            bi = [0]

            def nxt():
                i = bi[0] % 2
                bi[0] += 1
                return i

            for hf in range(2):
                i = nxt()
                P.dma("sp", dst32[i], lambda e, i=i, hf=hf: e.dma_start(
                    out=st32[i][:], in_=w_in_l[:, 4 * hf:4 * hf + 4, :].rearrange("p k n -> p (k n)")), writes=[B32[i]])
                P.task("dve", [(lambda e, kk=kk, i=i, hf=hf: e.tensor_scalar(
                    stb[i][:, kk * INW:(kk + 1) * INW], st32[i][:, kk * INW:(kk + 1) * INW],
                    gattn[:, 4 * hf + kk:4 * hf + kk + 1], None, op0=ALU.mult)) for kk in range(4)],
                    reads=[B32[i], B_const], writes=[Bb[i]])
                P.dma("sp", dstb[i], lambda e, i=i, hf=hf: e.dma_start(
                    out=win_s[:, 4 * hf * INW:(4 * hf + 4) * INW], in_=stb[i][:]), reads=[Bb[i]], writes=[B_win_parts[hf]])
            i = nxt()
            P.dma("sp", dst32[i], [lambda e, i=i: e.dma_start(out=st32[i][:, 0:1536], in_=wq_l.rearrange("p k n -> p (k n)")),
                                   lambda e, i=i: e.dma_start(out=st32[i][:, 1536:3584], in_=wkv_l.rearrange("p k n -> p (k n)"))],
                  writes=[B32[i]])
            P.task("dve", [(lambda e, k=k, i=i: e.tensor_scalar(wq_sb[:, k, :], st32[i][:, k * 768:(k + 1) * 768],
                                                                gq[:, k:k + 1], None, op0=ALU.mult)) for k in range(2)] +
                          [(lambda e, k=k, i=i: e.tensor_scalar(wkv_sb[:, k, :], st32[i][:, 1536 + k * 1024:1536 + (k + 1) * 1024],
                                                                gkv[:, k:k + 1], None, op0=ALU.mult)) for k in range(2)],
                   reads=[B32[i], B_const], writes=[B_wq, B_wkv])
            P.dma("pool", dcast, [(lambda e, k=k: e.dma_start(out=wout_s[:, k * D:(k + 1) * D], in_=w_out_l[:, k, :]))
                                  for k in range(8)], writes=[B_wout_s])
            P.dma("pool", dcast, [(lambda e, j=j: e.dma_start(out=wd_s[j], in_=wd_l[j])) for j in range(NJ)], writes=[B_wd_s])
            JB = 6
            for (src, dst, bparts) in ((wg_l, wg_s, B_wg_parts), (wu_l, wu_s, B_wu_parts)):
                for j0 in range(0, NJ, JB):
                    nj = min(JB, NJ - j0)
                    bdst = bparts[j0 // JB]
                    i = nxt()
                    P.dma("sp", dst32[i], lambda e, src=src, j0=j0, nj=nj, i=i: e.dma_start(
                        out=st32[i][:, 0:nj * 1024].rearrange("p (j q) -> p j q", j=nj),
                        in_=src[j0:j0 + nj].rearrange("j p k m -> p j (k m)")), writes=[B32[i]])
                    P.task("dve", [(lambda e, k=k, nj=nj, i=i: e.tensor_scalar(
                        stb[i][:, 0:nj * 1024].rearrange("p (j k m) -> p j k m", j=nj, k=8)[:, :, k, :],
                        st32[i][:, 0:nj * 1024].rearrange("p (j k m) -> p j k m", j=nj, k=8)[:, :, k, :],
                        gffn[:, k:k + 1], None, op0=ALU.mult)) for k in range(8)],
                        reads=[B32[i], B_const], writes=[Bb[i]])
                    P.dma("sp", dstb[i], lambda e, dst=dst, j0=j0, nj=nj, i=i: e.dma_start(
                        out=dst[j0:j0 + nj].rearrange("j p q -> p j q"),
                        in_=stb[i][:, 0:nj * 1024].rearrange("p (j q) -> p j q", j=nj)), reads=[Bb[i]], writes=[bdst])
            P.barrier()

        def conv_emit(C, k, c):
            src = C["uwin"][:, c, k:k + 512]
            acc = C["acc"]
            if k == 0:
                P.task("dve", lambda e: e.tensor_scalar(
                    acc[:, c, :], src, cw[:, c, 0:1], cb[:, c:c + 1], op0=ALU.mult, op1=ALU.add),
                    reads=[C["Buwin"], B_const], writes=[C["Bacc"][c]])
            else:
                P.task("dve", lambda e: e.scalar_tensor_tensor(
                    acc[:, c, :], src, cw[:, c, k:k + 1], acc[:, c, :], op0=ALU.mult, op1=ALU.add),
                    reads=[C["Buwin"], B_const], writes=[C["Bacc"][c]])

        def win_load(C, b):
            o0 = NMETA + b * 512
            deps = [B_upd_pad] + [B_upd[g][c] for g in range(NB) if b - 1 <= g <= b + 1 for c in range(4)]
            P.dma("sp", C["dwin"], lambda e: e.dma_start(
                out=C["uwin"][:], in_=upre_d[:, :, o0:o0 + 542].rearrange("c p n -> p c n")),
                reads=deps, writes=[C["Buwin"]])

        def ln_emit(C, dst, bdst, spi):
            acc, csq, mean, rstd = C["acc"], C["csq"], C["lnm"], C["lnr"]
            Bacc, Bcsq, Blnm, Blnr = C["Bacc"], C["Bcsq"], C["Blnm"], C["Blnr"]
            for c in range(4):
                sl = c % 2
                P.task("act", lambda e, c=c, sl=sl: e.activation(out=csq[sl][:], in_=acc[:, c, :], func=AF.Square),
                       reads=[Bacc[c]], writes=[Bcsq[sl]])
                P.task("pe", lambda e, c=c: e.matmul(pp[spi][:, 0:512], lhsT=ones32[:], rhs=acc[:, c, :], start=(c == 0), stop=(c == 3)),
                       reads=[Bacc[c], B_ones], writes=[ppb[spi][0]])
                P.task("pe", lambda e, c=c, sl=sl: e.matmul(pp[spi][:, 512:1024], lhsT=ones32[:], rhs=csq[sl][:], start=(c == 0), stop=(c == 3)),
                       reads=[Bcsq[sl], B_ones], writes=[ppb[spi][1]])
            P.task("dve", lambda e: e.tensor_scalar(mean[:], pp[spi][:, 0:512], 1.0 / CCH, None, op0=ALU.mult),
                   reads=[ppb[spi][0]], writes=[Blnm])
            P.task("dve", lambda e: e.tensor_tensor(rstd[:], mean[:], mean[:], op=ALU.mult), reads=[Blnm], writes=[Blnr])
            P.task("dve", lambda e: e.scalar_tensor_tensor(rstd[:], pp[spi][:, 512:1024], 1.0 / CCH, rstd[:], op0=ALU.mult, op1=ALU.subtract),
                   reads=[ppb[spi][1]], writes=[Blnr])
            P.task("dve", lambda e: e.tensor_scalar(rstd[:], rstd[:], EPS, None, op0=ALU.add), reads=[], writes=[Blnr])
            P.task("act", lambda e: e.activation(out=rstd[:], in_=rstd[:], func=AF.Sqrt), reads=[], writes=[Blnr])
            P.task("dve", lambda e: e.reciprocal(rstd[:], rstd[:]), reads=[], writes=[Blnr])
            for c in range(4):
                P.task("dve", lambda e, c=c: e.tensor_tensor(acc[:, c, :], acc[:, c, :], mean[:], op=ALU.subtract),
                       reads=[Blnm], writes=[Bacc[c]])
                P.task("dve", lambda e, c=c: e.tensor_tensor(acc[:, c, :], acc[:, c, :], rstd[:], op=ALU.mult),
                       reads=[Blnr], writes=[Bacc[c]])
                P.task("act", lambda e, c=c: e.activation(out=dst[:, c, :], in_=acc[:, c, :], func=AF.Silu,
                                                          bias=lnb[:, c:c + 1], scale=lng[:, c:c + 1]),
                       reads=[Bacc[c], B_const], writes=[bdst])

        def allocC(stack, tag, csq=None):
            C = {}
            C["uwin"] = sbuf(stack, "uwin" + tag, [128, 4, 542], F32)
            C["acc"] = sbuf(stack, "acc" + tag, [128, 4, 512], F32)
            C["lnm"] = sbuf(stack, "lnm" + tag, [128, 512], F32)
            C["lnr"] = sbuf(stack, "lnr" + tag, [128, 512], F32)
            C["csq"] = csq if csq is not None else [sbuf(stack, "csq%s%d" % (tag, i), [128, 512], F32) for i in range(2)]
            C["Buwin"], C["Blnm"], C["Blnr"] = Buf(), Buf(), Buf()
            C["Bacc"] = [Buf() for _ in range(4)]
            C["Bcsq"] = [Buf(), Buf()]
            C["dwin"] = DS["duwin"]
            return C

        def phaseA(seq, A, meta_pass):
            w_in_sb, xt, xs, hnT, th, ust, cn, ckvT, ktm, krt = (A[k] for k in
                ("w_in_sb", "xt", "xs", "hnT", "th", "ust", "cn", "ckvT", "ktm", "krt"))
            Bx, Bxs, BhnT, Bth, Bup, Bcn, BckvT, Bktm_n, Bktm_r, Bkrt, Bwin = (A[k] for k in
                ("Bx", "Bxs", "BhnT", "Bth", "Bup", "Bcn", "BckvT", "Bktm_n", "Bktm_r", "Bkrt", "Bwin"))
            ngroups = 1 if meta_pass else NB
            np_ = NMETA if meta_pass else 128
            ncol = NMETA if meta_pass else 512

            def tiles_of(g):
                return [None] if meta_pass else list(range(4 * g, 4 * g + 4))

            gstat = {}

            def front(g):
                srow, bst = next_stat()
                gstat[g] = (srow, bst)
                tl_n = len(tiles_of(g))
                for tl, t in enumerate(tiles_of(g)):
                    src = meta if meta_pass else x[seq, t * 128:(t + 1) * 128, :]
                    P.dma("sp", A["dx"][tl], lambda e, tl=tl, src=src: e.dma_start(out=xt[tl][0:np_, :], in_=src),
                          writes=[Bx[tl]])
                    sl = tl % 2
                    P.task("act", lambda e, tl=tl, sl=sl, srow=srow: e.activation(
                        out=xs[sl][0:np_, :], in_=xt[tl][0:np_, :], func=AF.Square, accum_out=srow[0:np_, tl:tl + 1]),
                        reads=[Bx[tl]], writes=[Bxs[sl], bst])
                rstd_from_ssq(srow[0:np_, 0:tl_n], srow[0:np_, 4:4 + tl_n], D, bst, np_, tl_n)

            def mid(g):
                srow, bst = gstat[g]
                for tl, t in enumerate(tiles_of(g)):
                    sl = tl % 2
                    P.task("dve", lambda e, tl=tl, sl=sl, srow=srow: e.tensor_scalar(
                        xs[sl][0:np_, :], xt[tl][0:np_, :], srow[0:np_, 4 + tl:5 + tl], None, op0=ALU.mult),
                        reads=[Bx[tl], bst], writes=[Bxs[sl]])
                    mp = 2 + sl
                    P.task("pe", [(lambda e, c=c, sl=sl, mp=mp: e.matmul(
                        pp[mp][:, c * 128:c * 128 + np_], lhsT=xs[sl][0:np_, c * 128:(c + 1) * 128], rhs=ident[0:np_, 0:np_],
                        start=True, stop=True)) for c in range(8)],
                        reads=[Bxs[sl], B_const], writes=[ppb[mp][0], ppb[mp][1]])
                    if sl == 0:
                        P.task("act", lambda e, tl=tl, mp=mp: e.copy(
                            hnT[:, :, tl * 128:tl * 128 + np_],
                            pp[mp][:].rearrange("p (c t) -> p c t", c=8)[:, :, 0:np_]),
                            reads=[ppb[mp][0], ppb[mp][1]], writes=[BhnT[tl]])
                    else:
                        P.task("dve", lambda e, tl=tl, mp=mp: e.tensor_copy(
                            hnT[:, :, tl * 128:tl * 128 + np_],
                            pp[mp][:].rearrange("p (c t) -> p c t", c=8)[:, :, 0:np_]),
                            reads=[ppb[mp][0], ppb[mp][1]], writes=[BhnT[tl]])

            def valgate(g):
                for c in range(4):
                    vp = 1 + (c % 2)
                    P.task("pe", [(lambda e, k=k, c=c, vp=vp: e.matmul(
                        pp[vp][:, 512:512 + ncol], lhsT=w_in_sb[:, k, 512 + c * 128:512 + (c + 1) * 128], rhs=hnT[:, k, 0:ncol],
                        start=(k == 0), stop=(k == 7))) for k in range(8)],
                        reads=BhnT + [Bwin], writes=[ppb[vp][1]])
                    P.task("pe", [(lambda e, k=k, c=c, vp=vp: e.matmul(
                        pp[vp][:, 0:ncol], lhsT=w_in_sb[:, k, c * 128:(c + 1) * 128], rhs=hnT[:, k, 0:ncol],
                        start=(k == 0), stop=(k == 7))) for k in range(8)],
                        reads=BhnT + [Bwin], writes=[ppb[vp][0]])
                    sl = c % 2
                    P.task("act", lambda e, sl=sl, vp=vp: e.activation(out=th[sl][:, 0:ncol], in_=pp[vp][:, 512:512 + ncol], func=AF.Sigmoid),
                           reads=[ppb[vp][1]], writes=[Bth[sl]])
                    if meta_pass:
                        P.task("dve", lambda e, sl=sl, c=c, vp=vp: e.tensor_tensor(upre_meta[:, c, :], pp[vp][:, 0:ncol], th[sl][:, 0:ncol], op=ALU.mult),
                               reads=[ppb[vp][0], Bth[sl]], writes=[B_upm])
                    else:
                        c0 = 15 + NMETA + g * 512
                        P.task("dve", lambda e, sl=sl, vp=vp: e.tensor_tensor(ust[sl][:], pp[vp][:, 0:512], th[sl][:], op=ALU.mult),
                               reads=[ppb[vp][0], Bth[sl]], writes=[A["Bust"][sl]])
                        P.dma("sp", A["dust"][sl], lambda e, sl=sl, c=c, c0=c0: e.dma_start(out=upre_d[c, :, c0:c0 + 512], in_=ust[sl][:]),
                              reads=[A["Bust"][sl]], writes=[B_upd[g][c]])

            tstat = {}

            def cfront(g, tl):
                pz = 2 + (tl % 2)
                P.task("pe", [(lambda e, k=k: e.matmul(
                    pp[pz][0:np_, 0:512], lhsT=hnT[:, k, tl * 128:tl * 128 + np_], rhs=w_in_sb[:, k, 1024:1536],
                    start=(k == 0), stop=(k == 7))) for k in range(8)],
                    reads=[BhnT[tl], Bwin], writes=[ppb[pz][0]])
                P.task("pe", [(lambda e, k=k: e.matmul(
                    pp[pz][0:np_, 512:544], lhsT=hnT[:, k, tl * 128:tl * 128 + np_], rhs=w_in_sb[:, k, 1536:1568],
                    start=(k == 0), stop=(k == 7))) for k in range(8)],
                    reads=[BhnT[tl], Bwin], writes=[ppb[pz][1]])
                srow, bst = next_stat()
                tstat[(g, tl)] = (srow, bst)
                sl = tl % 2
                P.task("act", [lambda e: e.activation(
                    out=cn[sl][0:np_, 0:256], in_=pp[pz][0:np_, 0:256], func=AF.Square, accum_out=srow[0:np_, 0:1]),
                    lambda e: e.activation(
                    out=cn[sl][0:np_, 256:512], in_=pp[pz][0:np_, 256:512], func=AF.Square, accum_out=srow[0:np_, 1:2])],
                    reads=[ppb[pz][0]], writes=[Bcn[sl], bst])
                rstd_from_ssq(srow[0:np_, 0:2], srow[0:np_, 2:4], QL, bst, np_, 2)

            def cback(g, tl, hook=None):
                t = tiles_of(g)[tl]
                pz = 2 + (tl % 2)
                sl = tl % 2
                srow, bst = tstat[(g, tl)]
                kt = 0 if meta_pass else t + 1
                kc0 = 0 if meta_pass else NMETA + t * 128
                cs_ap = cs_meta[:, :] if meta_pass else cs_tok[:, t, :]
                sn_ap = sn_meta[:, :] if meta_pass else sn_tok[:, t, :]
                P.task("act", [lambda e: e.mul(cn[sl][0:np_, 0:256], pp[pz][0:np_, 0:256], srow[0:np_, 2:3]),
                               lambda e: e.mul(cn[sl][0:np_, 256:512], pp[pz][0:np_, 256:512], srow[0:np_, 3:4])],
                       reads=[ppb[pz][0], bst], writes=[Bcn[sl]])
                P.task("dve", [
                    lambda e: e.tensor_tensor(krt[0:np_, 32:64], pp[pz][0:np_, 512:544], cs_ap[0:np_, :], op=ALU.mult),
                    lambda e: e.tensor_tensor(krt[0:np_, 64:80], pp[pz][0:np_, 528:544], sn_ap[0:np_, 0:16], op=ALU.mult),
                    lambda e: e.tensor_tensor(krt[0:np_, 80:96], pp[pz][0:np_, 512:528], sn_ap[0:np_, 16:32], op=ALU.mult)],
                    reads=[ppb[pz][1], B_const], writes=[Bkrt])
                P.task("dve", lambda e: e.tensor_tensor(krt[0:np_, 0:32], krt[0:np_, 32:64], krt[0:np_, 64:96], op=ALU.add),
                       reads=[Bkrt], writes=[Bkrt])
                P.task("dve", lambda e: e.tensor_copy(
                    ktm[0:np_, :, DN:DN + DR], krt[0:np_, 0:32].unsqueeze(1).to_broadcast([np_, NH, DR])),
                    reads=[Bkrt], writes=[Bktm_r])
                P.task("pe", [(lambda e, c=c: e.matmul(
                    pp[0][:, c * 128:c * 128 + np_], lhsT=cn[sl][0:np_, c * 128:(c + 1) * 128], rhs=ident[0:np_, 0:np_],
                    start=True, stop=True)) for c in range(4)],
                    reads=[Bcn[sl], B_const], writes=[ppb[0][0]])
                fns = [lambda e: e.copy(ckvT[:, :, 0:np_], pp[0][:, 256:512].rearrange("p (c t) -> p c t", c=2)[:, :, 0:np_])]
                wr = [BckvT]
                if not meta_pass:
                    fns.append(lambda e: e.copy(cnTq[:, :, t * 128:(t + 1) * 128], pp[0][:, 0:256].rearrange("p (c t) -> p c t", c=2)))
                    wr.append(B_cnTq[t])
                P.task("act", fns, reads=[ppb[0][0]], writes=wr)
                if hook is not None:
                    hook()
                for half in range(2):
                    P.task("pe", [(lambda e, kc=kc, half=half: e.matmul(
                        pp[1][0:np_, half * 512:(half + 1) * 512], lhsT=ckvT[:, kc, 0:np_], rhs=wkv_sb[:, kc, half * 512:(half + 1) * 512],
                        start=(kc == 0), stop=(kc == 1))) for kc in range(2)],
                        reads=[BckvT, B_wkv], writes=[ppb[1][half]])
                kvv = pp[1][:].rearrange("p (h d) -> p h d", h=NH)
                P.task("act", [lambda e: e.copy(ktm[0:np_, :, 0:DN], kvv[0:np_, :, 0:DN]),
                               lambda e: e.copy(Vsb[0:np_, kt, 0:NH * (DV + 1)].rearrange("p (h d) -> p h d", h=NH)[:, :, 0:DV], kvv[0:np_, :, DN:DN + DV])],
                       reads=[ppb[1][0], ppb[1][1]], writes=[Bktm_n, B_V[kt]])
                P.task("pe", [(lambda e, h=h: e.matmul(
                    pp[0][0:96, h * 128:h * 128 + np_], lhsT=ktm[0:np_, h, :], rhs=ident[0:np_, 0:np_],
                    start=True, stop=True)) for h in range(NH)],
                    reads=[Bktm_n, Bktm_r, B_const], writes=[ppb[0][0], ppb[0][1]])
                P.task("act", lambda e: e.copy(
                    KT[0:96, :, kc0:kc0 + np_], pp[0][0:96, :].rearrange("p (h t) -> p h t", h=NH)[:, :, 0:np_]),
                    reads=[ppb[0][0], ppb[0][1]], writes=[B_KT[kt]])

            CA = A.get("C")
            conv_q = []

            def conv_drain(n=2):
                saved, P.after_dve = P.after_dve, None
                for _ in range(min(n, len(conv_q))):
                    k, c = conv_q.pop(0)
                    conv_emit(CA, k, c)
                P.after_dve = saved

            front(0)
            mid(0)
            g_need = min(1, ngroups - 1)
            for g in range(ngroups):
                nt = len(tiles_of(g))
                if g + 1 < ngroups:
                    front(g + 1)
                valgate(g)
                if not meta_pass and g == g_need:
                    win_load(CA, 0)
                    conv_q.extend((k, c) for k in range(CK) for c in range(4))
                    P.after_dve = conv_drain
                cfront(g, 0)
                if nt > 1:
                    cfront(g, 1)
                for tl in range(nt):
                    cback(g, tl, (lambda tl=tl: cfront(g, tl + 2)) if tl + 2 < nt else None)
                if g + 1 < ngroups:
                    mid(g + 1)
            P.after_dve = None
            if meta_pass:
                P.dma("sp", DS["dupm"], [
                    lambda e: e.dma_start(out=upre_d[:, :, 15:15 + NMETA].rearrange("c p n -> p c n"), in_=upre_meta[:]),
                    lambda e: e.dma_start(out=upre_d[:, :, 0:15].rearrange("c p n -> p c n"), in_=zpad[:, :, 0:15]),
                    lambda e: e.dma_start(out=upre_d[:, :, 15 + L:UPW].rearrange("c p n -> p c n"), in_=zpad[:, :, 0:15])],
                    reads=[B_upm, B_ones], writes=[B_upd_pad])
            else:
                pending0[:] = conv_q

        def phaseB(seq, Bt):
            (w_out_sb, QT, qtm, qrt, pT, onT, rs, bcs, h2, hs, hfT, wgu, wdb, sg, aT) = (Bt[k] for k in
                ("w_out_sb", "QT", "qtm", "qrt", "pT", "onT", "rs", "bcs", "h2", "hs", "hfT", "wgu", "wdb", "sg", "aT"))
            CB = Bt["C"]
            CB["Bcsq"] = Bt["Bw"]["sg"]
            cq = []

            def cdrain(n):
                for _ in range(min(n, len(cq))):
                    k, c = cq.pop(0)
                    conv_emit(CB, k, c)
            Bw = Bt["Bw"]
            def q_front(b, tl, qi):
                t = 4 * b + tl
                sl = tl % 2
                P.task("pe", [(lambda e, kc=kc: e.matmul(
                    pp[qi][:, 0:480], lhsT=cnTq[:, kc, t * 128:(t + 1) * 128], rhs=wq_sb[:, kc, 0:480],
                    start=(kc == 0), stop=(kc == 1))) for kc in range(2)],
                    reads=[B_cnTq[t], B_wq], writes=[ppb[qi][0]])
                P.task("pe", [(lambda e, kc=kc: e.matmul(
                    pp[qi][:, 512:800], lhsT=cnTq[:, kc, t * 128:(t + 1) * 128], rhs=wq_sb[:, kc, 480:768],
                    start=(kc == 0), stop=(kc == 1))) for kc in range(2)],
                    reads=[B_cnTq[t], B_wq], writes=[ppb[qi][1]])
                qa = pp[qi][:, 0:480].rearrange("p (h d) -> p h d", h=5)
                qb = pp[qi][:, 512:800].rearrange("p (h d) -> p h d", h=3)
                P.task("act", [lambda e: e.copy(qtm[sl][:, 0:5, 0:DN], qa[:, :, 0:DN]),
                               lambda e: e.copy(qtm[sl][:, 5:8, 0:DN], qb[:, :, 0:DN])],
                       reads=[ppb[qi][0], ppb[qi][1]], writes=[Bw["qtm_n"][sl]])
                fns = []
                for (qv, h0, nh) in ((qa, 0, 5), (qb, 5, 3)):
                    csb = cs_tok[:, t, :].unsqueeze(1).to_broadcast([128, nh, 32])
                    snb = sn_tok[:, t, :].unsqueeze(1).to_broadcast([128, nh, 32])
                    fns.append(lambda e, qv=qv, h0=h0, nh=nh, csb=csb: e.tensor_tensor(
                        qrt[:, h0:h0 + nh, 0:32], qv[:, :, DN:DN + 32], csb, op=ALU.mult))
                    fns.append(lambda e, qv=qv, h0=h0, nh=nh, snb=snb: e.tensor_tensor(
                        qrt[:, h0:h0 + nh, 32:48], qv[:, :, DN + 16:DN + 32], snb[:, :, 0:16], op=ALU.mult))
                    fns.append(lambda e, qv=qv, h0=h0, nh=nh, snb=snb: e.tensor_tensor(
                        qrt[:, h0:h0 + nh, 48:64], qv[:, :, DN:DN + 16], snb[:, :, 16:32], op=ALU.mult))
                P.task("dve", fns, reads=[ppb[qi][0], ppb[qi][1], B_const], writes=[Bw["qrt"]])
                P.task("dve", lambda e: e.tensor_tensor(qtm[sl][:, :, DN:DN + DR], qrt[:, :, 0:32], qrt[:, :, 32:64], op=ALU.add),
                       reads=[Bw["qrt"]], writes=[Bw["qtm_r"][sl]])

            def q_back(b, tl, ti):
                sl = tl % 2
                P.task("pe", [(lambda e, h=h: e.matmul(
                    pp[ti][0:96, h * 128:(h + 1) * 128], lhsT=qtm[sl][:, h, :], rhs=ident[:], start=True, stop=True))
                    for h in range(NH)],
                    reads=[Bw["qtm_n"][sl], Bw["qtm_r"][sl], B_const], writes=[ppb[ti][0], ppb[ti][1]])
                P.task("act", lambda e: e.copy(
                    QT[0:96, :, tl * 128:(tl + 1) * 128], pp[ti][0:96, :].rearrange("p (h t) -> p h t", h=NH)),
                    reads=[ppb[ti][0], ppb[ti][1]], writes=[Bw["QT"][tl]])

            def qstage(b):
                for tl in range(4):
                    q_front(b, tl, 0)
                    q_back(b, tl, 1)

            P.task("pool", lambda e: e.memset(QT[96:128, :, :], 0.0), writes=Bw["QT"])
            for b in range(NB):
                P.dma("sp", Bt["dwout"], lambda e: e.dma_start(out=w_out_sb.rearrange("p k n -> p (k n)"), in_=wout_s),
                      reads=[B_wout_s], writes=[Bw["w_out"]] + Bw["aT"])
                if b == 0:
                    for (k_, c_) in pending0:
                        conv_emit(CB, k_, c_)
                    pending0[:] = []
                    ln_emit(CB, uTb[0], B_uTb[0], 2)
                    qstage(0)
                if b + 1 < NB:
                    win_load(CB, b + 1)
                    cq.extend((k, c) for k in range(CK) for c in range(4))
                groups = [[0]] + [[1 + 2 * i, 2 + 2 * i] for i in range(NT // 2)]

                def kinfo(kt):
                    if kt == 0:
                        return NMETA, 0
                    return 128, NMETA + (kt - 1) * 128

                items = [(h, gi) for h in range(NH) for gi in range(len(groups))]

                def emit_qk(idx):
                    h, gi = items[idx]
                    grp = groups[gi]
                    sp_i = 1 + (idx % 3)
                    psl = idx % 4
                    fns = []
                    for ii, kt in enumerate(grp):
                        nk, k0 = kinfo(kt)
                        fns.append(lambda e, ii=ii, nk=nk, k0=k0: e.matmul(
                            pp[sp_i][0:nk, ii * 512:(ii + 1) * 512], lhsT=KT[:, h, k0:k0 + nk], rhs=QT[:, h, :],
                            start=True, stop=True))
                    P.task("pe", fns, reads=[B_KT[kt] for kt in grp] + Bw["QT"],
                           writes=[ppb[sp_i][ii] for ii in range(len(grp))])
                    nkg = kinfo(grp[0])[0]
                    wcols = 512 * len(grp)
                    P.task("act", lambda e: e.activation(
                        out=pT[psl][0:nkg, 0:wcols], in_=pp[sp_i][0:nkg, 0:wcols], func=AF.Exp, scale=ATTN_SCALE),
                        reads=[ppb[sp_i][ii] for ii in range(len(grp))], writes=[Bw["pT"][psl]])

                def emit_pv(idx):
                    h, gi = items[idx]
                    grp = groups[gi]
                    psl = idx % 4
                    ob = ppb[0][h % 2]
                    o_ps = pp[0][:, (h % 2) * 512:(h % 2) * 512 + 512]
                    first = (gi == 0)
                    last_g = (gi == len(groups) - 1)
                    fns2 = []
                    for ii, kt in enumerate(grp):
                        nk, _ = kinfo(kt)
                        fns2.append(lambda e, ii=ii, kt=kt, nk=nk: e.matmul(
                            o_ps[:, :], lhsT=Vsb[0:nk, kt, h * (DV + 1):h * (DV + 1) + 128], rhs=pT[psl][0:nk, ii * 512:(ii + 1) * 512],
                            start=(first and ii == 0), stop=(last_g and ii == len(grp) - 1)))
                    P.task("pe", fns2, reads=[B_V[kt] for kt in grp] + [Bw["pT"][psl]], writes=[ob])
                    if not last_g:
                        return
                    P.task("dve", lambda e: e.reciprocal(rs[DV:DV + 1, :], o_ps[DV:DV + 1, :]),
                           reads=[ob], writes=[Bw["rs"]])

                    def tail():
                        P.task("pe", lambda e: e.matmul(o_ps[DV:2 * DV, :], lhsT=ones32[DV:DV + 1, 0:DV], rhs=rs[DV:DV + 1, :],
                                                        start=True, stop=True),
                               reads=[Bw["rs"], B_ones], writes=[ob])
                        P.task("act", lambda e: e.copy(bcs[0:DV, :], o_ps[DV:2 * DV, :]), reads=[ob], writes=[Bw["bcs"]])
                        po = (h % 2) * DV
                        P.task("dve", lambda e: e.tensor_tensor(
                            onT[po:po + DV, h // 2, :], o_ps[0:DV, :], bcs[0:DV, :], op=ALU.mult),
                            reads=[ob, Bw["bcs"]], writes=[Bw["onT"][h]])
                    return tail

                emit_qk(0)
                emit_qk(1)
                tails = []
                for idx in range(len(items)):
                    if idx + 2 < len(items):
                        emit_qk(idx + 2)
                    t_ = emit_pv(idx)
                    if t_ is not None:
                        tails.append((idx + 5, t_))
                    while tails and tails[0][0] <= idx:
                        tails.pop(0)[1]()
                    if idx >= 18:
                        cdrain(1)
                for _, t_ in tails:
                    t_()
                srow, bst = next_stat()
                for tl in range(4):
                    t = 4 * b + tl
                    sl = tl % 2
                    pi = 2 + sl
                    P.dma("sp", Bt["dx"][sl], lambda e, tl=tl, t=t: e.dma_start(out=h2[:, tl, :], in_=x[seq, t * 128:(t + 1) * 128, :]),
                          writes=[Bw["h2"][tl]])
                    for half in range(2):
                        fns = []
                        for c in range(4):
                            fns.append(lambda e, c=c, half=half, t=t, pi=pi: e.matmul(
                                pp[pi][:, half * 512:(half + 1) * 512], lhsT=uTb[b % 2][:, c, tl * 128:(tl + 1) * 128],
                                rhs=w_out_sb[:, c, half * 512:(half + 1) * 512], start=(c == 0), stop=False))
                        P.task("pe", fns, reads=[B_uTb[b % 2], Bw["w_out"]], writes=[ppb[pi][half]])
                    for half in range(2):
                        fns = []
                        for j in range(4):
                            fns.append(lambda e, j=j, half=half, tl=tl, pi=pi: e.matmul(
                                pp[pi][:, half * 512:(half + 1) * 512], lhsT=onT[:, j, tl * 128:(tl + 1) * 128],
                                rhs=w_out_sb[:, 4 + j, half * 512:(half + 1) * 512], start=False, stop=(j == 3)))
                        P.task("pe", fns, reads=[Bw["w_out"]] + Bw["onT"], writes=[ppb[pi][half]])
                    P.task("dve", lambda e, tl=tl, pi=pi: e.tensor_tensor(h2[:, tl, :], pp[pi][:], h2[:, tl, :], op=ALU.add),
                           reads=[ppb[pi][0], ppb[pi][1]], writes=[Bw["h2"][tl]])
                    P.task("act", lambda e, tl=tl, sl=sl, srow=srow: e.activation(
                        out=hs[sl][:], in_=h2[:, tl, :], func=AF.Square, accum_out=srow[:, tl:tl + 1]),
                        reads=[Bw["h2"][tl]], writes=[Bw["hs"][sl], bst])
                rstd_from_ssq(srow[:, 0:4], srow[:, 4:8], D, bst, 128, 4)
                for tl in range(4):
                    sl = tl % 2
                    P.task("dve", lambda e, tl=tl, sl=sl, srow=srow: e.tensor_scalar(
                        hs[sl][:], h2[:, tl, :], srow[:, 4 + tl:5 + tl], None, op0=ALU.mult),
                        reads=[Bw["h2"][tl], bst], writes=[Bw["hs"][sl]])
                    pj = sl
                    P.task("pe", [(lambda e, c=c, sl=sl, pj=pj: e.matmul(
                        pp[pj][:, c * 128:(c + 1) * 128], lhsT=hs[sl][:, c * 128:(c + 1) * 128], rhs=ident[:],
                        start=True, stop=True)) for c in range(8)],
                        reads=[Bw["hs"][sl], B_const], writes=[ppb[pj][0], ppb[pj][1]])
                    P.task("act", lambda e, tl=tl, pj=pj: e.copy(
                        hfT[:, :, tl * 128:(tl + 1) * 128], pp[pj][:].rearrange("p (c t) -> p c t", c=8)),
                        reads=[ppb[pj][0], ppb[pj][1]], writes=[Bw["hfT"][tl]])
                for j in range(NJ):
                    ws = j % 2
                    P.dma("sp", Bt["dwgu"][ws], [lambda e, j=j, ws=ws: e.dma_start(out=wgu[ws][:, 0, :], in_=wg_s[j]),
                                                   lambda e, j=j, ws=ws: e.dma_start(out=wgu[ws][:, 1, :], in_=wu_s[j])],
                          reads=B_wg_parts + B_wu_parts, writes=[Bw["wgu"][ws]])
                    pi = j % 2
                    for gu in range(2):
                        P.task("pe", [(lambda e, k=k, gu=gu, ws=ws, pi=pi: e.matmul(
                            pp[pi][:, gu * 512:(gu + 1) * 512], lhsT=wgu[ws][:, gu, k * 128:(k + 1) * 128], rhs=hfT[:, k, :],
                            start=(k == 0), stop=(k == 7))) for k in range(8)],
                            reads=[Bw["wgu"][ws]] + Bw["hfT"], writes=[ppb[pi][gu]])
                    sl = j % 2
                    P.task("act", lambda e, sl=sl, pi=pi: e.activation(out=sg[sl][:], in_=pp[pi][:, 0:512], func=AF.Silu),
                           reads=[ppb[pi][0]], writes=[Bw["sg"][sl]])
                    P.task("dve", lambda e, sl=sl, pi=pi, j=j: e.tensor_tensor(aT[:, j, :], pp[pi][:, 512:1024], sg[sl][:], op=ALU.mult),
                           reads=[ppb[pi][1], Bw["sg"][sl]], writes=[Bw["aT"][j], Bw["w_out"]])
                    cdrain(4)
                    if b + 1 < NB:
                        if j in (1, 6, 11, 16):
                            q_front(b + 1, (j - 1) // 5, 2)
                        if j in (4, 9, 14, 19):
                            q_back(b + 1, (j - 4) // 5, 3)
                cdrain(len(cq))
                if b + 1 < NB:
                    ln_emit(CB, uTb[(b + 1) % 2], B_uTb[(b + 1) % 2], 2)
                for j in range(NJ):
                    ws = j % 3
                    P.dma("sp", Bt["dwd"][ws], lambda e, j=j, ws=ws: e.dma_start(out=wdb[ws][:], in_=wd_s[j]),
                          reads=[B_wd_s], writes=[Bw["wd"][ws]])
                    fns = []
                    for tl in range(4):
                        for half in range(2):
                            fns.append(lambda e, tl=tl, half=half, j=j, ws=ws: e.matmul(
                                pp[tl][:, half * 512:(half + 1) * 512], lhsT=aT[:, j, tl * 128:(tl + 1) * 128],
                                rhs=wdb[ws][:, half * 512:(half + 1) * 512], start=(j == 0), stop=(j == NJ - 1)))
                    P.task("pe", fns, reads=[Bw["aT"][j], Bw["wd"][ws]], writes=[ppb[i][hh] for i in range(4) for hh in range(2)])
                srow, bst = next_stat()
                for tl in range(4):
                    sl = tl % 2
                    P.task("dve", lambda e, tl=tl: e.tensor_tensor(h2[:, tl, :], pp[tl][:], h2[:, tl, :], op=ALU.add),
                           reads=[ppb[tl][0], ppb[tl][1]], writes=[Bw["h2"][tl]])
                    P.task("act", lambda e, sl=sl, tl=tl, srow=srow: e.activation(
                        out=hs[sl][:], in_=h2[:, tl, :], func=AF.Square, accum_out=srow[:, tl:tl + 1]),
                        reads=[Bw["h2"][tl]], writes=[Bw["hs"][sl], bst])
                rstd_from_ssq(srow[:, 0:4], srow[:, 4:8], D, bst, 128, 4)
                for tl in range(4):
                    t = 4 * b + tl
                    sl = tl % 2
                    P.task("dve", lambda e, tl=tl, srow=srow: e.scalar_tensor_tensor(
                        h2[:, tl, :], h2[:, tl, :], srow[:, 4 + tl:5 + tl], gfin[:], op0=ALU.mult, op1=ALU.mult),
                        reads=[bst, B_const], writes=[Bw["h2"][tl]])
                    P.dma("sp", Bt["dout"][sl], lambda e, tl=tl, t=t: e.dma_start(out=out[seq, t * 128:(t + 1) * 128, :], in_=h2[:, tl, :]),
                          reads=[Bw["h2"][tl]])

        def allocA(stack, meta_pass):
            A = {}
            A["w_in_sb"] = sbuf(stack, "w_in_sb", [128, 8, INW], BF16)
            A["xt"] = [sbuf(stack, "xtA%d" % i, [128, D], F32) for i in range(4)]
            A["xs"] = [sbuf(stack, "xsA%d" % i, [128, D], BF16) for i in range(2)]
            A["hnT"] = sbuf(stack, "hnT", [128, 8, 512], BF16)
            A["th"] = [sbuf(stack, "th%d" % i, [128, 512], F32) for i in range(2)]
            A["ust"] = [sbuf(stack, "ust%d" % i, [128, 512], F32) for i in range(2)]
            A["Bust"] = [Buf(), Buf()]
            A["dust"] = DS["dust"]
            A["cn"] = [sbuf(stack, "cn%d" % i, [128, 512], BF16) for i in range(2)]
            A["ckvT"] = sbuf(stack, "ckvT", [128, 2, 128], BF16)
            A["ktm"] = sbuf(stack, "ktm", [128, NH, DN + DR], BF16)
            A["krt"] = sbuf(stack, "krt", [128, 96], F32)
            A["Bx"] = [Buf() for _ in range(4)]
            A["Bxs"] = [Buf(), Buf()]
            A["BhnT"] = [Buf() for _ in range(4)]
            A["Bth"] = [Buf(), Buf()]
            A["Bup"] = [[Buf() for _ in range(4)] for _ in range(NB)]
            A["Bup_pad"] = Buf()
            A["Bcn"] = [Buf(), Buf()]
            A["BckvT"] = Buf()
            A["Bktm_n"] = Buf()
            A["Bktm_r"] = Buf()
            A["Bkrt"] = Buf()
            A["Bwin"] = Buf()
            A["dx"] = DS["dxA"]
            A["dwin"] = DS["dwin"]
            if not meta_pass:
                A["C"] = C_all
                C_all["csq"], C_all["Bcsq"] = A["th"], A["Bth"]
            return A

        def allocB(stack):
            Bt = {}
            Bt["QT"] = sbuf(stack, "QT", [128, NH, 512], BF16)
            Bt["qtm"] = [sbuf(stack, "qtm%d" % i, [128, NH, DN + DR], BF16) for i in range(2)]
            Bt["qrt"] = sbuf(stack, "qrt", [128, NH, 64], F32)
            Bt["pT"] = [sbuf(stack, "pT%d" % i, [128, 1024], BF16) for i in range(4)]
            Bt["onT"] = sbuf(stack, "onT", [128, 4, 512], BF16)
            Bt["bcs"] = sbuf(stack, "bcs", [128, 512], F32)
            Bt["rs"] = Bt["bcs"]
            Bt["h2"] = sbuf(stack, "h2", [128, 4, D], F32)
            Bt["hs"] = [sbuf(stack, "hs%d" % i, [128, D], BF16) for i in range(2)]
            Bt["hfT"] = sbuf(stack, "hfT", [128, 8, 512], BF16)
            Bt["wgu"] = [sbuf(stack, "wgu%d" % i, [128, 2, 1024], BF16) for i in range(2)]
            Bt["wdb"] = [sbuf(stack, "wdb%d" % i, [128, D], BF16) for i in range(3)]
            Bt["C"] = None
            Bt["sg"] = [sbuf(stack, "sg%d" % i, [128, 512], F32) for i in range(2)]
            Bt["aT"] = sbuf(stack, "aT", [128, NJ, 512], BF16)
            Bt["w_out_sb"] = Bt["aT"][:, 0:16, :].rearrange("p a b -> p (a b)").rearrange("p (k n) -> p k n", k=8)
            Bt["Bw"] = {
                "w_out": Buf(), "QT": [Buf() for _ in range(4)], "qtm_n": [Buf(), Buf()], "qtm_r": [Buf(), Buf()],
                "qrt": Buf(), "pT": [Buf() for _ in range(4)], "onT": [Buf() for _ in range(NH)], "rs": Buf(), "bcs": Buf(),
                "xt": [Buf(), Buf()], "h2": [Buf() for _ in range(4)], "hs": [Buf(), Buf()], "hfT": [Buf() for _ in range(4)],
                "wgu": [Buf() for _ in range(3)], "wd": [Buf() for _ in range(3)], "sg": [Buf(), Buf()],
                "aT": [Buf() for _ in range(NJ)], "yt": [Buf(), Buf()], "ot": [Buf(), Buf()],
            }
            Bt["C"] = C_all
            C_all["csq"] = Bt["sg"]
            Bt["dx"] = DS["dxB"]
            Bt["dwgu"] = DS["dwgu"]
            Bt["dwd"] = DS["dwd"]
            Bt["dout"] = DS["dout"]
            Bt["dwout"] = DS["dwout"]
            return Bt

        DS = {
            "dxA": [P.new_dma_sem("dxA%d" % i) for i in range(4)], "dwin": P.new_dma_sem("dwin"),
            "dxB": [P.new_dma_sem("dxB0"), P.new_dma_sem("dxB1")],
            "dwgu": [P.new_dma_sem("dwgu%d" % i) for i in range(3)],
            "dwd": [P.new_dma_sem("dwd%d" % i) for i in range(3)],
            "dout": [P.new_dma_sem("dout0"), P.new_dma_sem("dout1")], "dwout": P.new_dma_sem("dwout"),
            "dust": [P.new_dma_sem("dust0"), P.new_dma_sem("dust1")], "duwin": P.new_dma_sem("duwin"), "dupm": P.new_dma_sem("dupm"),
        }
        C_all = allocC(st, "P", csq=[None, None])
        pending0 = []
        with ExitStack() as s1:
            A = allocA(s1, True)
            P.dma("sp", A["dwin"], lambda e: e.dma_start(out=A["w_in_sb"][:].rearrange("p k n -> p (k n)"), in_=win_s),
                  reads=B_win_parts, writes=[A["Bwin"]])
            if stop >= 2 and stop != 3.5:
                phaseA(0, A, True)
            P.barrier()

        for seq in range(NSEQ):
            with ExitStack() as s1:
                A = allocA(s1, False)
                P.dma("sp", A["dwin"], lambda e: e.dma_start(out=A["w_in_sb"][:].rearrange("p k n -> p (k n)"), in_=win_s),
                      reads=B_win_parts, writes=[A["Bwin"]])
                if stop >= 3:
                    phaseA(seq, A, False)
                P.barrier()
            with ExitStack() as s1:
                Bt = allocB(s1)
                if stop >= 5:
                    phaseB(seq, Bt)
                P.barrier()
        P.barrier()
    return nc


def _rope_tables(S):
    inv = (1.0 / (np.float32(10000.0) ** (np.arange(0, DR, 2, dtype=np.float32) / np.float32(DR)))).astype(np.float32)
    pos = np.arange(S + NMETA, dtype=np.float32)
    ang = (pos[:, None] * inv[None, :]).astype(np.float32)
    cos = np.cos(ang).astype(np.float32)
    sin = np.sin(ang).astype(np.float32)
    cs = np.concatenate([cos, cos], axis=1)
    sn = np.concatenate([-sin, sin], axis=1)
    NT = S // 128
    cs_tok = np.ascontiguousarray(cs[NMETA:].reshape(NT, 128, 32).transpose(1, 0, 2))
    sn_tok = np.ascontiguousarray(sn[NMETA:].reshape(NT, 128, 32).transpose(1, 0, 2))
    return cs_tok, sn_tok, np.ascontiguousarray(cs[:NMETA]), np.ascontiguousarray(sn[:NMETA])


def _layout_weights(meta_tokens, attn_norm_g, w_in, q_norm_g, w_q_up, kv_norm_g, w_kv_up, conv_dw_w, conv_dw_b,
                    conv_ln_g, conv_ln_b, w_out, ffn_norm_g, w_gate, w_up, w_down, final_norm_g, S):
    f = lambda a: np.ascontiguousarray(np.asarray(a, dtype=np.float32))
    cs_tok, sn_tok, cs_meta, sn_meta = _rope_tables(S)
    vec = lambda v, k: f(np.asarray(v).reshape(k, 128).T)
    return {
        "meta": f(meta_tokens),
        "w_in_l": f(np.asarray(w_in[0]).reshape(8, 128, INW).transpose(1, 0, 2)),
        "wq_l": f(np.asarray(w_q_up[0]).reshape(2, 128, 768).transpose(1, 0, 2)),
        "wkv_l": f(np.asarray(w_kv_up[0]).reshape(2, 128, 1024).transpose(1, 0, 2)),
        "w_out_l": f(np.asarray(w_out[0]).reshape(8, 128, D).transpose(1, 0, 2)),
        "wg_l": f(np.asarray(w_gate[0]).reshape(8, 128, NJ, 128).transpose(2, 1, 0, 3)),
        "wu_l": f(np.asarray(w_up[0]).reshape(8, 128, NJ, 128).transpose(2, 1, 0, 3)),
        "wd_l": f(np.asarray(w_down[0]).reshape(NJ, 128, D)),
        "gattn_l": vec(attn_norm_g[0], 8),
        "gq_l": vec(q_norm_g[0], 2),
        "gkv_l": vec(kv_norm_g[0], 2),
        "gffn_l": vec(ffn_norm_g[0], 8),
        "cw_l": f(np.asarray(conv_dw_w[0]).T.reshape(4, 128, CK).transpose(1, 0, 2)),
        "cb_l": vec(conv_dw_b[0], 4),
        "lng_l": vec(conv_ln_g[0], 4),
        "lnb_l": vec(conv_ln_b[0], 4),
        "gfin_l": f(np.broadcast_to(np.asarray(final_norm_g)[None, :], (128, D))),
        "ident_l": np.eye(128, dtype=np.float32).astype(ml_dtypes.bfloat16),
        "cs_tok_l": cs_tok, "sn_tok_l": sn_tok, "cs_meta_l": cs_meta, "sn_meta_l": sn_meta,
    }


_NC_CACHE = {}


def kernel(x_prompt, x_sample, meta_tokens, attn_norm_g, w_in, q_norm_g, w_q_up, kv_norm_g, w_kv_up,
           conv_dw_w, conv_dw_b, conv_ln_g, conv_ln_b, w_out, ffn_norm_g, w_gate, w_up, w_down, final_norm_g):
    x_prompt = np.asarray(x_prompt, dtype=np.float32)
    x_sample = np.asarray(x_sample, dtype=np.float32)
    nb_p, S, _ = x_prompt.shape
    nb_s = x_sample.shape[0]
    assert x_sample.shape[1] == S
    ntot = nb_p + nb_s
    assert ntot % N_CORES == 0
    nseq = ntot // N_CORES
    key = (nseq, S)
    if key not in _NC_CACHE:
        _NC_CACHE[key] = build(nseq, S)
    nc = _NC_CACHE[key]
    wl = _layout_weights(meta_tokens, attn_norm_g, w_in, q_norm_g, w_q_up, kv_norm_g, w_kv_up, conv_dw_w, conv_dw_b,
                         conv_ln_g, conv_ln_b, w_out, ffn_norm_g, w_gate, w_up, w_down, final_norm_g, S)
    seqs = [x_prompt[i] for i in range(nb_p)] + [x_sample[i] for i in range(nb_s)]
    in_maps = []
    for c in range(N_CORES):
        m = dict(wl)
        m["x"] = np.ascontiguousarray(np.stack(seqs[c * nseq:(c + 1) * nseq], axis=0))
        in_maps.append(m)
    res = run_bass_kernel_spmd(nc, in_maps, core_ids=list(range(N_CORES)))
    outs = [res.results[c]["out"] for c in range(N_CORES)]
    allo = np.concatenate(outs, axis=0)
    return (np.ascontiguousarray(allo[:nb_p]), np.ascontiguousarray(allo[nb_p:]))
```

```python
import math
from contextlib import ExitStack

import numpy as np
import ml_dtypes
import concourse.bass as bass
import concourse.mybir as mybir
from concourse.bass_utils import run_bass_kernel_spmd

F32 = mybir.dt.float32
BF16 = mybir.dt.bfloat16
AF = mybir.ActivationFunctionType
ALU = mybir.AluOpType

D = 1024
NMETA = 16
CCH = 512
CK = 31
NH = 8
DN = 64
DR = 32
DV = 64
QL = 256
DFF = 2816
NJ = DFF // 128
INW = 1568
EPS = 1e-6
ATTN_SCALE = 1.0 / math.sqrt(DN + DR)
N_CORES = 8


class Buf:
    __slots__ = ("name", "writers", "readers", "excl")

    def __init__(self, name="", excl=False):
        self.name = name
        self.excl = excl
        self.writers = {}
        self.readers = {}


def _merge(d, tok):
    k = tok[2]
    if k not in d or d[k][1] < tok[1]:
        d[k] = tok


class Prog:
    ENGS = ("pe", "act", "dve", "pool", "sp")

    def __init__(self, nc, stack):
        self.nc = nc
        self.stack = stack
        self.eng = {"pe": nc.tensor, "act": nc.scalar, "dve": nc.vector, "pool": nc.gpsimd, "sp": nc.sync}
        self.sem = {e: stack.enter_context(nc.semaphore("s_" + e)) for e in self.ENGS}
        self.cnt = {e: 0 for e in self.ENGS}
        self.waited = {e: {} for e in self.ENGS}
        self.dma_sems = []
        self.after_dve = None

    def _waits_for(self, eng, toks):
        best = {}
        for t in toks:
            sem, val, key, src = t
            if src == "pe" and eng == "pe":
                continue
            if key not in best or best[key][1] < val:
                best[key] = (sem, val)
        w = self.waited[eng]
        out = []
        for key, (sem, val) in best.items():
            if w.get(key, -1) >= val:
                continue
            w[key] = val
            out.append((sem, val))
        return out

    def _deps(self, reads, writes, extra):
        toks = list(extra)
        for b in reads:
            toks += list(b.writers.values())
            if b.excl:
                toks += list(b.readers.values())
        for b in writes:
            toks += list(b.writers.values())
            toks += list(b.readers.values())
        return toks

    def _emit(self, eng, waits, fns, inc, each):
        e = self.eng[eng]
        for sem, val in waits:
            e.wait_ge(sem, val)
        n = len(fns)
        for i, f in enumerate(fns):
            ins = f(e)
            if each or i == n - 1:
                ins.then_inc(inc[0], inc[1])

    def _record(self, tok, reads, writes):
        for b in reads:
            _merge(b.readers, tok)
        for b in writes:
            b.writers = {tok[2]: tok}
            b.readers = {}

    def task(self, eng, fns, reads=(), writes=(), extra=()):
        if callable(fns):
            fns = [fns]
        waits = self._waits_for(eng, self._deps(reads, writes, extra))
        self.cnt[eng] += 1
        tok = (self.sem[eng], self.cnt[eng], eng, eng)
        self._emit(eng, waits, fns, (self.sem[eng], 1), False)
        self._record(tok, reads, writes)
        hook = self.after_dve
        if eng == "dve" and hook is not None:
            self.after_dve = None
            hook()
            self.after_dve = hook
        return tok

    def new_dma_sem(self, name):
        s = self.stack.enter_context(self.nc.semaphore(name))
        st = {"sem": s, "val": 0, "key": "dma%d_%s" % (len(self.dma_sems), name)}
        self.dma_sems.append(st)
        return st

    def dma(self, eng, dsem, fns, reads=(), writes=(), extra=()):
        if callable(fns):
            fns = [fns]
        waits = self._waits_for(eng, self._deps(reads, writes, extra))
        dsem["val"] += 16 * len(fns)
        tok = (dsem["sem"], dsem["val"], dsem["key"], None)
        self._emit(eng, waits, fns, (dsem["sem"], 16), True)
        self._record(tok, reads, writes)
        return tok

    def wait_all(self, eng, toks):
        e = self.eng[eng]
        for sem, val in self._waits_for(eng, toks):
            e.wait_ge(sem, val)

    def barrier(self):
        toks = [(self.sem[e], self.cnt[e], e, e) for e in self.ENGS if self.cnt[e] > 0]
        toks += [(d["sem"], d["val"], d["key"], None) for d in self.dma_sems if d["val"] > 0]
        for e in self.ENGS:
            self.wait_all(e, [t for t in toks if t[3] != e])


def build(NSEQ, S, stop=99):
    assert S % 512 == 0
    NT = S // 128
    NB = S // 512
    L = S + NMETA
    NKT = NT + 1
    UPW = S + NMETA + 30

    nc = bass.Bass("TRN2", target_bir_lowering=False)

    def din(name, shape, dt=F32):
        return nc.dram_tensor(name, list(shape), dt, kind="ExternalInput").ap()

    x = din("x", [NSEQ, S, D])
    meta = din("meta", [NMETA, D])
    w_in_l = din("w_in_l", [128, 8, INW])
    wq_l = din("wq_l", [128, 2, 768])
    wkv_l = din("wkv_l", [128, 2, 1024])
    w_out_l = din("w_out_l", [128, 8, D])
    wg_l = din("wg_l", [NJ, 128, 8, 128])
    wu_l = din("wu_l", [NJ, 128, 8, 128])
    wd_l = din("wd_l", [NJ, 128, D])
    gattn_l = din("gattn_l", [128, 8])
    gq_l = din("gq_l", [128, 2])
    gkv_l = din("gkv_l", [128, 2])
    gffn_l = din("gffn_l", [128, 8])
    cw_l = din("cw_l", [128, 4, CK])
    cb_l = din("cb_l", [128, 4])
    lng_l = din("lng_l", [128, 4])
    lnb_l = din("lnb_l", [128, 4])
    gfin_l = din("gfin_l", [128, D])
    ident_l = din("ident_l", [128, 128], BF16)
    cs_tok_l = din("cs_tok_l", [128, NT, 32])
    sn_tok_l = din("sn_tok_l", [128, NT, 32])
    cs_meta_l = din("cs_meta_l", [NMETA, 32])
    sn_meta_l = din("sn_meta_l", [NMETA, 32])
    out = nc.dram_tensor("out", [NSEQ, S, D], F32, kind="ExternalOutput").ap()

    win_s = nc.dram_tensor("win_s", [128, 8 * INW], BF16).ap()
    wout_s = nc.dram_tensor("wout_s", [128, 8 * D], BF16).ap()
    wg_s = nc.dram_tensor("wg_s", [NJ, 128, 1024], BF16).ap()
    wu_s = nc.dram_tensor("wu_s", [NJ, 128, 1024], BF16).ap()
    wd_s = nc.dram_tensor("wd_s", [NJ, 128, D], BF16).ap()
    upre_d = nc.dram_tensor("upre_d", [4, 128, UPW], F32).ap()

    with ExitStack() as st:
        P = Prog(nc, st)

        uid = [0]

        def sbuf(stack, name, shape, dt):
            uid[0] += 1
            return stack.enter_context(nc.sbuf_tensor("%s_%d" % (name, uid[0]), list(shape), dt))

        pp = [st.enter_context(nc.psum_tensor("pp%d" % i, [128, 1024], F32)) for i in range(4)]
        ppb = [[Buf("pp%d_%d" % (i, h), True) for h in range(2)] for i in range(4)]

        ident = sbuf(st, "ident", [128, 128], BF16)
        ones32 = sbuf(st, "ones32", [128, 128], F32)
        neghalf = sbuf(st, "neghalf", [128, 8], F32)
        zpad = sbuf(st, "zpad", [128, 4, 16], F32)
        gattn = sbuf(st, "gattn", [128, 8], F32)
        gq = sbuf(st, "gq", [128, 2], F32)
        gkv = sbuf(st, "gkv", [128, 2], F32)
        gffn = sbuf(st, "gffn", [128, 8], F32)
        cw = sbuf(st, "cw", [128, 4, CK], F32)
        cb = sbuf(st, "cb", [128, 4], F32)
        lng = sbuf(st, "lng", [128, 4], F32)
        lnb = sbuf(st, "lnb", [128, 4], F32)
        gfin = sbuf(st, "gfin", [128, D], F32)
        cs_tok = sbuf(st, "cs_tok", [128, NT, 32], F32)
        sn_tok = sbuf(st, "sn_tok", [128, NT, 32], F32)
        cs_meta = sbuf(st, "cs_meta", [NMETA, 32], F32)
        sn_meta = sbuf(st, "sn_meta", [NMETA, 32], F32)
        wq_sb = sbuf(st, "wq_sb", [128, 2, 768], BF16)
        wkv_sb = sbuf(st, "wkv_sb", [128, 2, 1024], BF16)
        KT = sbuf(st, "KT", [128, NH, L], BF16)
        Vsb = sbuf(st, "Vsb", [128, NKT, NH * (DV + 1) + 64], BF16)
        uTb = [sbuf(st, "uTb%d" % i, [128, 4, 512], BF16) for i in range(2)]
        cnTq = sbuf(st, "cnTq", [128, 2, S], BF16)
        upre_meta = sbuf(st, "upre_meta", [128, 4, NMETA], F32)
        stats = sbuf(st, "stats", [128, 8, 8], F32)

        B_const = Buf("const")
        B_wq = Buf("wq")
        B_wkv = Buf("wkv")
        B_KT = [Buf("KT%d" % i) for i in range(NKT)]
        B_V = [Buf("V%d" % i) for i in range(NKT)]
        B_uTb = [Buf("uTb0"), Buf("uTb1")]
        B_upd = [[Buf("upd%d_%d" % (g, c)) for c in range(4)] for g in range(NB)]
        B_upd_pad = Buf("upd_pad")
        B_cnTq = [Buf("cnTq%d" % i) for i in range(NT)]
        B_upm = Buf("upre_meta")
        B_stats = [Buf("stats%d" % i) for i in range(8)]
        B_win_s = Buf("win_s")
        B_win_parts = [Buf("win_s0"), Buf("win_s1")]
        B_wg_parts = [Buf("wg_s%d" % i) for i in range(4)]
        B_wu_parts = [Buf("wu_s%d" % i) for i in range(4)]
        B_wout_s = Buf("wout_s")
        B_wg_s = Buf("wg_s")
        B_wu_s = Buf("wu_s")
        B_wd_s = Buf("wd_s")
        stat_rr = [0]

        def next_stat():
            i = stat_rr[0] % 8
            stat_rr[0] += 1
            return stats[:, i, :], B_stats[i]

        dconst = P.new_dma_sem("dconst")
        dwq = P.new_dma_sem("dwq")
        dwkv = P.new_dma_sem("dwkv")

        consts = [(ident, ident_l), (gattn, gattn_l), (gq, gq_l), (gkv, gkv_l), (gffn, gffn_l), (cw, cw_l),
                  (cb, cb_l), (lng, lng_l), (lnb, lnb_l), (gfin, gfin_l), (cs_tok, cs_tok_l), (sn_tok, sn_tok_l),
                  (cs_meta, cs_meta_l), (sn_meta, sn_meta_l)]
        P.dma("sp", dconst, [(lambda e, d=d, s=s: e.dma_start(out=d[:], in_=s)) for d, s in consts], writes=[B_const])
        B_ones = Buf("ones")
        P.task("pool", [lambda e: e.memset(ones32[:], 1.0), lambda e: e.memset(neghalf[:], -0.5), lambda e: e.memset(zpad[:], 0.0),
                        lambda e: e.memset(Vsb[:], 0.0),
                        lambda e: e.memset(KT[96:128, :, :], 0.0)], writes=[B_ones] + B_V + B_KT)
        P.task("pool", lambda e: e.memset(
            Vsb[:, :, 0:NH * (DV + 1)].rearrange("p k (h d) -> p k h d", h=NH)[:, :, :, DV:DV + 1], 1.0), writes=B_V)

        def rstd_from_ssq(ssq_ap, out_ap, n, bst, np_=128, w=1):
            P.task("dve", lambda e: e.tensor_scalar(out_ap, ssq_ap, 1.0 / n, EPS, op0=ALU.mult, op1=ALU.add),
                   reads=[bst], writes=[bst])
            P.task("pool", lambda e: e.tensor_tensor(out_ap, out_ap, neghalf[0:np_, 0:w], op=ALU.pow),
                   reads=[bst, B_ones], writes=[bst])

        with ExitStack() as ps_:
            HW_ = 4 * INW
            st32 = [sbuf(ps_, "st32_%d" % i, [128, HW_], F32) for i in range(2)]
            stb = [sbuf(ps_, "stb_%d" % i, [128, HW_], BF16) for i in range(2)]
            B32 = [Buf("st32a"), Buf("st32b")]
            Bb = [Buf("stba"), Buf("stbb")]
            dst32 = [P.new_dma_sem("dst32a"), P.new_dma_sem("dst32b")]
            dstb = [P.new_dma_sem("dstba"), P.new_dma_sem("dstbb")]
            dcast = P.new_dma_sem("dcast")
            bi = [0]

            def nxt():
                i = bi[0] % 2
                bi[0] += 1
                return i

            for hf in range(2):
                i = nxt()
                P.dma("sp", dst32[i], lambda e, i=i, hf=hf: e.dma_start(
                    out=st32[i][:], in_=w_in_l[:, 4 * hf:4 * hf + 4, :].rearrange("p k n -> p (k n)")), writes=[B32[i]])
                P.task("dve", [(lambda e, kk=kk, i=i, hf=hf: e.tensor_scalar(
                    stb[i][:, kk * INW:(kk + 1) * INW], st32[i][:, kk * INW:(kk + 1) * INW],
                    gattn[:, 4 * hf + kk:4 * hf + kk + 1], None, op0=ALU.mult)) for kk in range(4)],
                    reads=[B32[i], B_const], writes=[Bb[i]])
                P.dma("sp", dstb[i], lambda e, i=i, hf=hf: e.dma_start(
                    out=win_s[:, 4 * hf * INW:(4 * hf + 4) * INW], in_=stb[i][:]), reads=[Bb[i]], writes=[B_win_parts[hf]])
            i = nxt()
            P.dma("sp", dst32[i], [lambda e, i=i: e.dma_start(out=st32[i][:, 0:1536], in_=wq_l.rearrange("p k n -> p (k n)")),
                                   lambda e, i=i: e.dma_start(out=st32[i][:, 1536:3584], in_=wkv_l.rearrange("p k n -> p (k n)"))],
                  writes=[B32[i]])
            P.task("dve", [(lambda e, k=k, i=i: e.tensor_scalar(wq_sb[:, k, :], st32[i][:, k * 768:(k + 1) * 768],
                                                                gq[:, k:k + 1], None, op0=ALU.mult)) for k in range(2)] +
                          [(lambda e, k=k, i=i: e.tensor_scalar(wkv_sb[:, k, :], st32[i][:, 1536 + k * 1024:1536 + (k + 1) * 1024],
                                                                gkv[:, k:k + 1], None, op0=ALU.mult)) for k in range(2)],
                   reads=[B32[i], B_const], writes=[B_wq, B_wkv])
            P.dma("pool", dcast, [(lambda e, k=k: e.dma_start(out=wout_s[:, k * D:(k + 1) * D], in_=w_out_l[:, k, :]))
                                  for k in range(8)], writes=[B_wout_s])
            P.dma("pool", dcast, [(lambda e, j=j: e.dma_start(out=wd_s[j], in_=wd_l[j])) for j in range(NJ)], writes=[B_wd_s])
            JB = 6
            for (src, dst, bparts) in ((wg_l, wg_s, B_wg_parts), (wu_l, wu_s, B_wu_parts)):
                for j0 in range(0, NJ, JB):
                    nj = min(JB, NJ - j0)
                    bdst = bparts[j0 // JB]
                    i = nxt()
                    P.dma("sp", dst32[i], lambda e, src=src, j0=j0, nj=nj, i=i: e.dma_start(
                        out=st32[i][:, 0:nj * 1024].rearrange("p (j q) -> p j q", j=nj),
                        in_=src[j0:j0 + nj].rearrange("j p k m -> p j (k m)")), writes=[B32[i]])
                    P.task("dve", [(lambda e, k=k, nj=nj, i=i: e.tensor_scalar(
                        stb[i][:, 0:nj * 1024].rearrange("p (j k m) -> p j k m", j=nj, k=8)[:, :, k, :],
                        st32[i][:, 0:nj * 1024].rearrange("p (j k m) -> p j k m", j=nj, k=8)[:, :, k, :],
                        gffn[:, k:k + 1], None, op0=ALU.mult)) for k in range(8)],
                        reads=[B32[i], B_const], writes=[Bb[i]])
                    P.dma("sp", dstb[i], lambda e, dst=dst, j0=j0, nj=nj, i=i: e.dma_start(
                        out=dst[j0:j0 + nj].rearrange("j p q -> p j q"),
                        in_=stb[i][:, 0:nj * 1024].rearrange("p (j q) -> p j q", j=nj)), reads=[Bb[i]], writes=[bdst])
            P.barrier()

        def conv_emit(C, k, c):
            src = C["uwin"][:, c, k:k + 512]
            acc = C["acc"]
            if k == 0:
                P.task("dve", lambda e: e.tensor_scalar(
                    acc[:, c, :], src, cw[:, c, 0:1], cb[:, c:c + 1], op0=ALU.mult, op1=ALU.add),
                    reads=[C["Buwin"], B_const], writes=[C["Bacc"][c]])
            else:
                P.task("dve", lambda e: e.scalar_tensor_tensor(
                    acc[:, c, :], src, cw[:, c, k:k + 1], acc[:, c, :], op0=ALU.mult, op1=ALU.add),
                    reads=[C["Buwin"], B_const], writes=[C["Bacc"][c]])

        def win_load(C, b):
            o0 = NMETA + b * 512
            deps = [B_upd_pad] + [B_upd[g][c] for g in range(NB) if b - 1 <= g <= b + 1 for c in range(4)]
            P.dma("sp", C["dwin"], lambda e: e.dma_start(
                out=C["uwin"][:], in_=upre_d[:, :, o0:o0 + 542].rearrange("c p n -> p c n")),
                reads=deps, writes=[C["Buwin"]])

        def ln_emit(C, dst, bdst, spi):
            acc, csq, mean, rstd = C["acc"], C["csq"], C["lnm"], C["lnr"]
            Bacc, Bcsq, Blnm, Blnr = C["Bacc"], C["Bcsq"], C["Blnm"], C["Blnr"]
            for c in range(4):
                sl = c % 2
                P.task("act", lambda e, c=c, sl=sl: e.activation(out=csq[sl][:], in_=acc[:, c, :], func=AF.Square),
                       reads=[Bacc[c]], writes=[Bcsq[sl]])
                P.task("pe", lambda e, c=c: e.matmul(pp[spi][:, 0:512], lhsT=ones32[:], rhs=acc[:, c, :], start=(c == 0), stop=(c == 3)),
                       reads=[Bacc[c], B_ones], writes=[ppb[spi][0]])
                P.task("pe", lambda e, c=c, sl=sl: e.matmul(pp[spi][:, 512:1024], lhsT=ones32[:], rhs=csq[sl][:], start=(c == 0), stop=(c == 3)),
                       reads=[Bcsq[sl], B_ones], writes=[ppb[spi][1]])
            P.task("dve", lambda e: e.tensor_scalar(mean[:], pp[spi][:, 0:512], 1.0 / CCH, None, op0=ALU.mult),
                   reads=[ppb[spi][0]], writes=[Blnm])
            P.task("dve", lambda e: e.tensor_tensor(rstd[:], mean[:], mean[:], op=ALU.mult), reads=[Blnm], writes=[Blnr])
            P.task("dve", lambda e: e.scalar_tensor_tensor(rstd[:], pp[spi][:, 512:1024], 1.0 / CCH, rstd[:], op0=ALU.mult, op1=ALU.subtract),
                   reads=[ppb[spi][1]], writes=[Blnr])
            P.task("dve", lambda e: e.tensor_scalar(rstd[:], rstd[:], EPS, None, op0=ALU.add), reads=[], writes=[Blnr])
            P.task("act", lambda e: e.activation(out=rstd[:], in_=rstd[:], func=AF.Sqrt), reads=[], writes=[Blnr])
            P.task("dve", lambda e: e.reciprocal(rstd[:], rstd[:]), reads=[], writes=[Blnr])
            for c in range(4):
                P.task("dve", lambda e, c=c: e.tensor_tensor(acc[:, c, :], acc[:, c, :], mean[:], op=ALU.subtract),
                       reads=[Blnm], writes=[Bacc[c]])
                P.task("dve", lambda e, c=c: e.tensor_tensor(acc[:, c, :], acc[:, c, :], rstd[:], op=ALU.mult),
                       reads=[Blnr], writes=[Bacc[c]])
                P.task("act", lambda e, c=c: e.activation(out=dst[:, c, :], in_=acc[:, c, :], func=AF.Silu,
                                                          bias=lnb[:, c:c + 1], scale=lng[:, c:c + 1]),
                       reads=[Bacc[c], B_const], writes=[bdst])

        def allocC(stack, tag, csq=None):
            C = {}
            C["uwin"] = sbuf(stack, "uwin" + tag, [128, 4, 542], F32)
            C["acc"] = sbuf(stack, "acc" + tag, [128, 4, 512], F32)
            C["lnm"] = sbuf(stack, "lnm" + tag, [128, 512], F32)
            C["lnr"] = sbuf(stack, "lnr" + tag, [128, 512], F32)
            C["csq"] = csq if csq is not None else [sbuf(stack, "csq%s%d" % (tag, i), [128, 512], F32) for i in range(2)]
            C["Buwin"], C["Blnm"], C["Blnr"] = Buf(), Buf(), Buf()
            C["Bacc"] = [Buf() for _ in range(4)]
            C["Bcsq"] = [Buf(), Buf()]
            C["dwin"] = DS["duwin"]
            return C

        def phaseA(seq, A, meta_pass):
            w_in_sb, xt, xs, hnT, th, ust, cn, ckvT, ktm, krt = (A[k] for k in
                ("w_in_sb", "xt", "xs", "hnT", "th", "ust", "cn", "ckvT", "ktm", "krt"))
            Bx, Bxs, BhnT, Bth, Bup, Bcn, BckvT, Bktm_n, Bktm_r, Bkrt, Bwin = (A[k] for k in
                ("Bx", "Bxs", "BhnT", "Bth", "Bup", "Bcn", "BckvT", "Bktm_n", "Bktm_r", "Bkrt", "Bwin"))
            ngroups = 1 if meta_pass else NB
            np_ = NMETA if meta_pass else 128
            ncol = NMETA if meta_pass else 512

            def tiles_of(g):
                return [None] if meta_pass else list(range(4 * g, 4 * g + 4))

            gstat = {}

            def front(g):
                srow, bst = next_stat()
                gstat[g] = (srow, bst)
                tl_n = len(tiles_of(g))
                for tl, t in enumerate(tiles_of(g)):
                    src = meta if meta_pass else x[seq, t * 128:(t + 1) * 128, :]
                    P.dma("sp", A["dx"][tl], lambda e, tl=tl, src=src: e.dma_start(out=xt[tl][0:np_, :], in_=src),
                          writes=[Bx[tl]])
                    sl = tl % 2
                    P.task("act", lambda e, tl=tl, sl=sl, srow=srow: e.activation(
                        out=xs[sl][0:np_, :], in_=xt[tl][0:np_, :], func=AF.Square, accum_out=srow[0:np_, tl:tl + 1]),
                        reads=[Bx[tl]], writes=[Bxs[sl], bst])
                rstd_from_ssq(srow[0:np_, 0:tl_n], srow[0:np_, 4:4 + tl_n], D, bst, np_, tl_n)

            def mid(g):
                srow, bst = gstat[g]
                for tl, t in enumerate(tiles_of(g)):
                    sl = tl % 2
                    P.task("dve", lambda e, tl=tl, sl=sl, srow=srow: e.tensor_scalar(
                        xs[sl][0:np_, :], xt[tl][0:np_, :], srow[0:np_, 4 + tl:5 + tl], None, op0=ALU.mult),
                        reads=[Bx[tl], bst], writes=[Bxs[sl]])
                    mp = 2 + sl
                    P.task("pe", [(lambda e, c=c, sl=sl, mp=mp: e.matmul(
                        pp[mp][:, c * 128:c * 128 + np_], lhsT=xs[sl][0:np_, c * 128:(c + 1) * 128], rhs=ident[0:np_, 0:np_],
                        start=True, stop=True)) for c in range(8)],
                        reads=[Bxs[sl], B_const], writes=[ppb[mp][0], ppb[mp][1]])
                    if sl == 0:
                        P.task("act", lambda e, tl=tl, mp=mp: e.copy(
                            hnT[:, :, tl * 128:tl * 128 + np_],
                            pp[mp][:].rearrange("p (c t) -> p c t", c=8)[:, :, 0:np_]),
                            reads=[ppb[mp][0], ppb[mp][1]], writes=[BhnT[tl]])
                    else:
                        P.task("dve", lambda e, tl=tl, mp=mp: e.tensor_copy(
                            hnT[:, :, tl * 128:tl * 128 + np_],
                            pp[mp][:].rearrange("p (c t) -> p c t", c=8)[:, :, 0:np_]),
                            reads=[ppb[mp][0], ppb[mp][1]], writes=[BhnT[tl]])

            def valgate(g):
                for c in range(4):
                    vp = 1 + (c % 2)
                    P.task("pe", [(lambda e, k=k, c=c, vp=vp: e.matmul(
                        pp[vp][:, 512:512 + ncol], lhsT=w_in_sb[:, k, 512 + c * 128:512 + (c + 1) * 128], rhs=hnT[:, k, 0:ncol],
                        start=(k == 0), stop=(k == 7))) for k in range(8)],
                        reads=BhnT + [Bwin], writes=[ppb[vp][1]])
                    P.task("pe", [(lambda e, k=k, c=c, vp=vp: e.matmul(
                        pp[vp][:, 0:ncol], lhsT=w_in_sb[:, k, c * 128:(c + 1) * 128], rhs=hnT[:, k, 0:ncol],
                        start=(k == 0), stop=(k == 7))) for k in range(8)],
                        reads=BhnT + [Bwin], writes=[ppb[vp][0]])
                    sl = c % 2
                    P.task("act", lambda e, sl=sl, vp=vp: e.activation(out=th[sl][:, 0:ncol], in_=pp[vp][:, 512:512 + ncol], func=AF.Sigmoid),
                           reads=[ppb[vp][1]], writes=[Bth[sl]])
                    if meta_pass:
                        P.task("dve", lambda e, sl=sl, c=c, vp=vp: e.tensor_tensor(upre_meta[:, c, :], pp[vp][:, 0:ncol], th[sl][:, 0:ncol], op=ALU.mult),
                               reads=[ppb[vp][0], Bth[sl]], writes=[B_upm])
                    else:
                        c0 = 15 + NMETA + g * 512
                        P.task("dve", lambda e, sl=sl, vp=vp: e.tensor_tensor(ust[sl][:], pp[vp][:, 0:512], th[sl][:], op=ALU.mult),
                               reads=[ppb[vp][0], Bth[sl]], writes=[A["Bust"][sl]])
                        P.dma("sp", A["dust"][sl], lambda e, sl=sl, c=c, c0=c0: e.dma_start(out=upre_d[c, :, c0:c0 + 512], in_=ust[sl][:]),
                              reads=[A["Bust"][sl]], writes=[B_upd[g][c]])

            tstat = {}

            def cfront(g, tl):
                pz = 2 + (tl % 2)
                P.task("pe", [(lambda e, k=k: e.matmul(
                    pp[pz][0:np_, 0:512], lhsT=hnT[:, k, tl * 128:tl * 128 + np_], rhs=w_in_sb[:, k, 1024:1536],
                    start=(k == 0), stop=(k == 7))) for k in range(8)],
                    reads=[BhnT[tl], Bwin], writes=[ppb[pz][0]])
                P.task("pe", [(lambda e, k=k: e.matmul(
                    pp[pz][0:np_, 512:544], lhsT=hnT[:, k, tl * 128:tl * 128 + np_], rhs=w_in_sb[:, k, 1536:1568],
                    start=(k == 0), stop=(k == 7))) for k in range(8)],
                    reads=[BhnT[tl], Bwin], writes=[ppb[pz][1]])
                srow, bst = next_stat()
                tstat[(g, tl)] = (srow, bst)
                sl = tl % 2
                P.task("act", [lambda e: e.activation(
                    out=cn[sl][0:np_, 0:256], in_=pp[pz][0:np_, 0:256], func=AF.Square, accum_out=srow[0:np_, 0:1]),
                    lambda e: e.activation(
                    out=cn[sl][0:np_, 256:512], in_=pp[pz][0:np_, 256:512], func=AF.Square, accum_out=srow[0:np_, 1:2])],
                    reads=[ppb[pz][0]], writes=[Bcn[sl], bst])
                rstd_from_ssq(srow[0:np_, 0:2], srow[0:np_, 2:4], QL, bst, np_, 2)

            def cback(g, tl, hook=None):
                t = tiles_of(g)[tl]
                pz = 2 + (tl % 2)
                sl = tl % 2
                srow, bst = tstat[(g, tl)]
                kt = 0 if meta_pass else t + 1
                kc0 = 0 if meta_pass else NMETA + t * 128
                cs_ap = cs_meta[:, :] if meta_pass else cs_tok[:, t, :]
                sn_ap = sn_meta[:, :] if meta_pass else sn_tok[:, t, :]
                P.task("act", [lambda e: e.mul(cn[sl][0:np_, 0:256], pp[pz][0:np_, 0:256], srow[0:np_, 2:3]),
                               lambda e: e.mul(cn[sl][0:np_, 256:512], pp[pz][0:np_, 256:512], srow[0:np_, 3:4])],
                       reads=[ppb[pz][0], bst], writes=[Bcn[sl]])
                P.task("dve", [
                    lambda e: e.tensor_tensor(krt[0:np_, 32:64], pp[pz][0:np_, 512:544], cs_ap[0:np_, :], op=ALU.mult),
                    lambda e: e.tensor_tensor(krt[0:np_, 64:80], pp[pz][0:np_, 528:544], sn_ap[0:np_, 0:16], op=ALU.mult),
                    lambda e: e.tensor_tensor(krt[0:np_, 80:96], pp[pz][0:np_, 512:528], sn_ap[0:np_, 16:32], op=ALU.mult)],
                    reads=[ppb[pz][1], B_const], writes=[Bkrt])
                P.task("dve", lambda e: e.tensor_tensor(krt[0:np_, 0:32], krt[0:np_, 32:64], krt[0:np_, 64:96], op=ALU.add),
                       reads=[Bkrt], writes=[Bkrt])
                P.task("dve", lambda e: e.tensor_copy(
                    ktm[0:np_, :, DN:DN + DR], krt[0:np_, 0:32].unsqueeze(1).to_broadcast([np_, NH, DR])),
                    reads=[Bkrt], writes=[Bktm_r])
                P.task("pe", [(lambda e, c=c: e.matmul(
                    pp[0][:, c * 128:c * 128 + np_], lhsT=cn[sl][0:np_, c * 128:(c + 1) * 128], rhs=ident[0:np_, 0:np_],
                    start=True, stop=True)) for c in range(4)],
                    reads=[Bcn[sl], B_const], writes=[ppb[0][0]])
                fns = [lambda e: e.copy(ckvT[:, :, 0:np_], pp[0][:, 256:512].rearrange("p (c t) -> p c t", c=2)[:, :, 0:np_])]
                wr = [BckvT]
                if not meta_pass:
                    fns.append(lambda e: e.copy(cnTq[:, :, t * 128:(t + 1) * 128], pp[0][:, 0:256].rearrange("p (c t) -> p c t", c=2)))
                    wr.append(B_cnTq[t])
                P.task("act", fns, reads=[ppb[0][0]], writes=wr)
                if hook is not None:
                    hook()
                for half in range(2):
                    P.task("pe", [(lambda e, kc=kc, half=half: e.matmul(
                        pp[1][0:np_, half * 512:(half + 1) * 512], lhsT=ckvT[:, kc, 0:np_], rhs=wkv_sb[:, kc, half * 512:(half + 1) * 512],
                        start=(kc == 0), stop=(kc == 1))) for kc in range(2)],
                        reads=[BckvT, B_wkv], writes=[ppb[1][half]])
                kvv = pp[1][:].rearrange("p (h d) -> p h d", h=NH)
                P.task("act", [lambda e: e.copy(ktm[0:np_, :, 0:DN], kvv[0:np_, :, 0:DN]),
                               lambda e: e.copy(Vsb[0:np_, kt, 0:NH * (DV + 1)].rearrange("p (h d) -> p h d", h=NH)[:, :, 0:DV], kvv[0:np_, :, DN:DN + DV])],
                       reads=[ppb[1][0], ppb[1][1]], writes=[Bktm_n, B_V[kt]])
                P.task("pe", [(lambda e, h=h: e.matmul(
                    pp[0][0:96, h * 128:h * 128 + np_], lhsT=ktm[0:np_, h, :], rhs=ident[0:np_, 0:np_],
                    start=True, stop=True)) for h in range(NH)],
                    reads=[Bktm_n, Bktm_r, B_const], writes=[ppb[0][0], ppb[0][1]])
                P.task("act", lambda e: e.copy(
                    KT[0:96, :, kc0:kc0 + np_], pp[0][0:96, :].rearrange("p (h t) -> p h t", h=NH)[:, :, 0:np_]),
                    reads=[ppb[0][0], ppb[0][1]], writes=[B_KT[kt]])

            CA = A.get("C")
            conv_q = []

            def conv_drain(n=2):
                saved, P.after_dve = P.after_dve, None
                for _ in range(min(n, len(conv_q))):
                    k, c = conv_q.pop(0)
                    conv_emit(CA, k, c)
                P.after_dve = saved

            front(0)
            if A.get("load_win") is not None:
                A["load_win"]()
            mid(0)
            g_need = min(1, ngroups - 1)
            for g in range(ngroups):
                nt = len(tiles_of(g))
                if g + 1 < ngroups:
                    front(g + 1)
                valgate(g)
                if not meta_pass and g == g_need:
                    win_load(CA, 0)
                    conv_q.extend((k, c) for k in range(CK) for c in range(4))
                    P.after_dve = conv_drain
                cfront(g, 0)
                if nt > 1:
                    cfront(g, 1)
                for tl in range(nt):
                    cback(g, tl, (lambda tl=tl: cfront(g, tl + 2)) if tl + 2 < nt else None)
                if g + 1 < ngroups:
                    mid(g + 1)
            P.after_dve = None
            if meta_pass:
                P.dma("sp", DS["dupm"], [
                    lambda e: e.dma_start(out=upre_d[:, :, 15:15 + NMETA].rearrange("c p n -> p c n"), in_=upre_meta[:]),
                    lambda e: e.dma_start(out=upre_d[:, :, 0:15].rearrange("c p n -> p c n"), in_=zpad[:, :, 0:15]),
                    lambda e: e.dma_start(out=upre_d[:, :, 15 + L:UPW].rearrange("c p n -> p c n"), in_=zpad[:, :, 0:15])],
                    reads=[B_upm, B_ones], writes=[B_upd_pad])
            else:
                pending0[:] = conv_q

        def phaseB(seq, Bt):
            (w_out_sb, QT, qtm, qrt, pT, onT, rs, bcs, h2, hs, hfT, wgu, wdb, sg, aT) = (Bt[k] for k in
                ("w_out_sb", "QT", "qtm", "qrt", "pT", "onT", "rs", "bcs", "h2", "hs", "hfT", "wgu", "wdb", "sg", "aT"))
            CB = Bt["C"]
            CB["Bcsq"] = Bt["Bw"]["sg"]
            cq = []

            def cdrain(n):
                for _ in range(min(n, len(cq))):
                    k, c = cq.pop(0)
                    conv_emit(CB, k, c)
            Bw = Bt["Bw"]
            def q_front(b, tl, qi):
                t = 4 * b + tl
                sl = tl % 2
                P.task("pe", [(lambda e, kc=kc: e.matmul(
                    pp[qi][:, 0:480], lhsT=cnTq[:, kc, t * 128:(t + 1) * 128], rhs=wq_sb[:, kc, 0:480],
                    start=(kc == 0), stop=(kc == 1))) for kc in range(2)],
                    reads=[B_cnTq[t], B_wq], writes=[ppb[qi][0]])
                P.task("pe", [(lambda e, kc=kc: e.matmul(
                    pp[qi][:, 512:800], lhsT=cnTq[:, kc, t * 128:(t + 1) * 128], rhs=wq_sb[:, kc, 480:768],
                    start=(kc == 0), stop=(kc == 1))) for kc in range(2)],
                    reads=[B_cnTq[t], B_wq], writes=[ppb[qi][1]])
                qa = pp[qi][:, 0:480].rearrange("p (h d) -> p h d", h=5)
                qb = pp[qi][:, 512:800].rearrange("p (h d) -> p h d", h=3)
                P.task("act", [lambda e: e.copy(qtm[sl][:, 0:5, 0:DN], qa[:, :, 0:DN]),
                               lambda e: e.copy(qtm[sl][:, 5:8, 0:DN], qb[:, :, 0:DN])],
                       reads=[ppb[qi][0], ppb[qi][1]], writes=[Bw["qtm_n"][sl]])
                fns = []
                for (qv, h0, nh) in ((qa, 0, 5), (qb, 5, 3)):
                    csb = cs_tok[:, t, :].unsqueeze(1).to_broadcast([128, nh, 32])
                    snb = sn_tok[:, t, :].unsqueeze(1).to_broadcast([128, nh, 32])
                    fns.append(lambda e, qv=qv, h0=h0, nh=nh, csb=csb: e.tensor_tensor(
                        qrt[:, h0:h0 + nh, 0:32], qv[:, :, DN:DN + 32], csb, op=ALU.mult))
                    fns.append(lambda e, qv=qv, h0=h0, nh=nh, snb=snb: e.tensor_tensor(
                        qrt[:, h0:h0 + nh, 32:48], qv[:, :, DN + 16:DN + 32], snb[:, :, 0:16], op=ALU.mult))
                    fns.append(lambda e, qv=qv, h0=h0, nh=nh, snb=snb: e.tensor_tensor(
                        qrt[:, h0:h0 + nh, 48:64], qv[:, :, DN:DN + 16], snb[:, :, 16:32], op=ALU.mult))
                P.task("dve", fns, reads=[ppb[qi][0], ppb[qi][1], B_const], writes=[Bw["qrt"]])
                P.task("dve", lambda e: e.tensor_tensor(qtm[sl][:, :, DN:DN + DR], qrt[:, :, 0:32], qrt[:, :, 32:64], op=ALU.add),
                       reads=[Bw["qrt"]], writes=[Bw["qtm_r"][sl]])

            def q_back(b, tl, ti):
                sl = tl % 2
                P.task("pe", [(lambda e, h=h: e.matmul(
                    pp[ti][0:96, h * 128:(h + 1) * 128], lhsT=qtm[sl][:, h, :], rhs=ident[:], start=True, stop=True))
                    for h in range(NH)],
                    reads=[Bw["qtm_n"][sl], Bw["qtm_r"][sl], B_const], writes=[ppb[ti][0], ppb[ti][1]])
                P.task("act", lambda e: e.copy(
                    QT[0:96, :, tl * 128:(tl + 1) * 128], pp[ti][0:96, :].rearrange("p (h t) -> p h t", h=NH)),
                    reads=[ppb[ti][0], ppb[ti][1]], writes=[Bw["QT"][tl]])

            def qstage(b):
                for tl in range(4):
                    q_front(b, tl, 0)
                    q_back(b, tl, 1)

            P.task("pool", lambda e: e.memset(QT[96:128, :, :], 0.0), writes=Bw["QT"])
            for b in range(NB):
                P.dma("sp", Bt["dwout"], lambda e: e.dma_start(out=w_out_sb.rearrange("p k n -> p (k n)"), in_=wout_s),
                      reads=[B_wout_s], writes=[Bw["w_out"]] + Bw["aT"])
                if b == 0:
                    for (k_, c_) in pending0:
                        conv_emit(CB, k_, c_)
                    pending0[:] = []
                    ln_emit(CB, uTb[0], B_uTb[0], 2)
                    qstage(0)
                if b + 1 < NB:
                    win_load(CB, b + 1)
                    cq.extend((k, c) for k in range(CK) for c in range(4))
                groups = [[0]] + [[1 + 2 * i, 2 + 2 * i] for i in range(NT // 2)]

                def kinfo(kt):
                    if kt == 0:
                        return NMETA, 0
                    return 128, NMETA + (kt - 1) * 128

                items = [(h, gi) for h in range(NH) for gi in range(len(groups))]

                def emit_qk(idx):
                    h, gi = items[idx]
                    grp = groups[gi]
                    sp_i = 1 + (idx % 3)
                    psl = idx % 4
                    fns = []
                    for ii, kt in enumerate(grp):
                        nk, k0 = kinfo(kt)
                        fns.append(lambda e, ii=ii, nk=nk, k0=k0: e.matmul(
                            pp[sp_i][0:nk, ii * 512:(ii + 1) * 512], lhsT=KT[:, h, k0:k0 + nk], rhs=QT[:, h, :],
                            start=True, stop=True))
                    P.task("pe", fns, reads=[B_KT[kt] for kt in grp] + Bw["QT"],
                           writes=[ppb[sp_i][ii] for ii in range(len(grp))])
                    nkg = kinfo(grp[0])[0]
                    wcols = 512 * len(grp)
                    P.task("act", lambda e: e.activation(
                        out=pT[psl][0:nkg, 0:wcols], in_=pp[sp_i][0:nkg, 0:wcols], func=AF.Exp, scale=ATTN_SCALE),
                        reads=[ppb[sp_i][ii] for ii in range(len(grp))], writes=[Bw["pT"][psl]])

                def emit_pv(idx):
                    h, gi = items[idx]
                    grp = groups[gi]
                    psl = idx % 4
                    ob = ppb[0][h % 2]
                    o_ps = pp[0][:, (h % 2) * 512:(h % 2) * 512 + 512]
                    first = (gi == 0)
                    last_g = (gi == len(groups) - 1)
                    fns2 = []
                    for ii, kt in enumerate(grp):
                        nk, _ = kinfo(kt)
                        fns2.append(lambda e, ii=ii, kt=kt, nk=nk: e.matmul(
                            o_ps[:, :], lhsT=Vsb[0:nk, kt, h * (DV + 1):h * (DV + 1) + 128], rhs=pT[psl][0:nk, ii * 512:(ii + 1) * 512],
                            start=(first and ii == 0), stop=(last_g and ii == len(grp) - 1)))
                    P.task("pe", fns2, reads=[B_V[kt] for kt in grp] + [Bw["pT"][psl]], writes=[ob])
                    if not last_g:
                        return
                    P.task("dve", lambda e: e.reciprocal(rs[DV:DV + 1, :], o_ps[DV:DV + 1, :]),
                           reads=[ob], writes=[Bw["rs"]])

                    def tail():
                        P.task("pe", lambda e: e.matmul(o_ps[DV:2 * DV, :], lhsT=ones32[DV:DV + 1, 0:DV], rhs=rs[DV:DV + 1, :],
                                                        start=True, stop=True),
                               reads=[Bw["rs"], B_ones], writes=[ob])
                        P.task("act", lambda e: e.copy(bcs[0:DV, :], o_ps[DV:2 * DV, :]), reads=[ob], writes=[Bw["bcs"]])
                        po = (h % 2) * DV
                        P.task("dve", lambda e: e.tensor_tensor(
                            onT[po:po + DV, h // 2, :], o_ps[0:DV, :], bcs[0:DV, :], op=ALU.mult),
                            reads=[ob, Bw["bcs"]], writes=[Bw["onT"][h]])
                    return tail

                emit_qk(0)
                emit_qk(1)
                tails = []
                for idx in range(len(items)):
                    if idx + 2 < len(items):
                        emit_qk(idx + 2)
                    t_ = emit_pv(idx)
                    if t_ is not None:
                        tails.append((idx + 5, t_))
                    while tails and tails[0][0] <= idx:
                        tails.pop(0)[1]()
                    if idx >= 18:
                        cdrain(1)
                for _, t_ in tails:
                    t_()
                srow, bst = next_stat()
                for tl in range(4):
                    t = 4 * b + tl
                    sl = tl % 2
                    pi = 2 + sl
                    P.dma("sp", Bt["dx"][sl], lambda e, tl=tl, t=t: e.dma_start(out=h2[:, tl, :], in_=x[seq, t * 128:(t + 1) * 128, :]),
                          writes=[Bw["h2"][tl]])
                    for half in range(2):
                        fns = []
                        for c in range(4):
                            fns.append(lambda e, c=c, half=half, t=t, pi=pi: e.matmul(
                                pp[pi][:, half * 512:(half + 1) * 512], lhsT=uTb[b % 2][:, c, tl * 128:(tl + 1) * 128],
                                rhs=w_out_sb[:, c, half * 512:(half + 1) * 512], start=(c == 0), stop=False))
                        P.task("pe", fns, reads=[B_uTb[b % 2], Bw["w_out"]], writes=[ppb[pi][half]])
                    for half in range(2):
                        fns = []
                        for j in range(4):
                            fns.append(lambda e, j=j, half=half, tl=tl, pi=pi: e.matmul(
                                pp[pi][:, half * 512:(half + 1) * 512], lhsT=onT[:, j, tl * 128:(tl + 1) * 128],
                                rhs=w_out_sb[:, 4 + j, half * 512:(half + 1) * 512], start=False, stop=(j == 3)))
                        P.task("pe", fns, reads=[Bw["w_out"]] + Bw["onT"], writes=[ppb[pi][half]])
                    P.task("dve", lambda e, tl=tl, pi=pi: e.tensor_tensor(h2[:, tl, :], pp[pi][:], h2[:, tl, :], op=ALU.add),
                           reads=[ppb[pi][0], ppb[pi][1]], writes=[Bw["h2"][tl]])
                    P.task("act", lambda e, tl=tl, sl=sl, srow=srow: e.activation(
                        out=hs[sl][:], in_=h2[:, tl, :], func=AF.Square, accum_out=srow[:, tl:tl + 1]),
                        reads=[Bw["h2"][tl]], writes=[Bw["hs"][sl], bst])
                rstd_from_ssq(srow[:, 0:4], srow[:, 4:8], D, bst, 128, 4)
                for tl in range(4):
                    sl = tl % 2
                    P.task("dve", lambda e, tl=tl, sl=sl, srow=srow: e.tensor_scalar(
                        hs[sl][:], h2[:, tl, :], srow[:, 4 + tl:5 + tl], None, op0=ALU.mult),
                        reads=[Bw["h2"][tl], bst], writes=[Bw["hs"][sl]])
                    pj = sl
                    P.task("pe", [(lambda e, c=c, sl=sl, pj=pj: e.matmul(
                        pp[pj][:, c * 128:(c + 1) * 128], lhsT=hs[sl][:, c * 128:(c + 1) * 128], rhs=ident[:],
                        start=True, stop=True)) for c in range(8)],
                        reads=[Bw["hs"][sl], B_const], writes=[ppb[pj][0], ppb[pj][1]])
                    P.task("act", lambda e, tl=tl, pj=pj: e.copy(
                        hfT[:, :, tl * 128:(tl + 1) * 128], pp[pj][:].rearrange("p (c t) -> p c t", c=8)),
                        reads=[ppb[pj][0], ppb[pj][1]], writes=[Bw["hfT"][tl]])
                for j in range(NJ):
                    ws = j % 2
                    P.dma("sp", Bt["dwgu"][ws], [lambda e, j=j, ws=ws: e.dma_start(out=wgu[ws][:, 0, :], in_=wg_s[j]),
                                                   lambda e, j=j, ws=ws: e.dma_start(out=wgu[ws][:, 1, :], in_=wu_s[j])],
                          reads=B_wg_parts + B_wu_parts, writes=[Bw["wgu"][ws]])
                    pi = j % 2
                    for gu in range(2):
                        P.task("pe", [(lambda e, k=k, gu=gu, ws=ws, pi=pi: e.matmul(
                            pp[pi][:, gu * 512:(gu + 1) * 512], lhsT=wgu[ws][:, gu, k * 128:(k + 1) * 128], rhs=hfT[:, k, :],
                            start=(k == 0), stop=(k == 7))) for k in range(8)],
                            reads=[Bw["wgu"][ws]] + Bw["hfT"], writes=[ppb[pi][gu]])
                    sl = j % 2
                    P.task("act", lambda e, sl=sl, pi=pi: e.activation(out=sg[sl][:], in_=pp[pi][:, 0:512], func=AF.Silu),
                           reads=[ppb[pi][0]], writes=[Bw["sg"][sl]])
                    P.task("dve", lambda e, sl=sl, pi=pi, j=j: e.tensor_tensor(aT[:, j, :], pp[pi][:, 512:1024], sg[sl][:], op=ALU.mult),
                           reads=[ppb[pi][1], Bw["sg"][sl]], writes=[Bw["aT"][j], Bw["w_out"]])
                    cdrain(4)
                    if b + 1 < NB:
                        if j in (1, 6, 11, 16):
                            q_front(b + 1, (j - 1) // 5, 2)
                        if j in (4, 9, 14, 19):
                            q_back(b + 1, (j - 4) // 5, 3)
                cdrain(len(cq))
                if b + 1 < NB:
                    ln_emit(CB, uTb[(b + 1) % 2], B_uTb[(b + 1) % 2], 2)
                for j in range(NJ):
                    ws = j % 3
                    P.dma("sp", Bt["dwd"][ws], lambda e, j=j, ws=ws: e.dma_start(out=wdb[ws][:], in_=wd_s[j]),
                          reads=[B_wd_s], writes=[Bw["wd"][ws]])
                    fns = []
                    for tl in range(4):
                        for half in range(2):
                            fns.append(lambda e, tl=tl, half=half, j=j, ws=ws: e.matmul(
                                pp[tl][:, half * 512:(half + 1) * 512], lhsT=aT[:, j, tl * 128:(tl + 1) * 128],
                                rhs=wdb[ws][:, half * 512:(half + 1) * 512], start=(j == 0), stop=(j == NJ - 1)))
                    P.task("pe", fns, reads=[Bw["aT"][j], Bw["wd"][ws]], writes=[ppb[i][hh] for i in range(4) for hh in range(2)])
                srow, bst = next_stat()
                for tl in range(4):
                    sl = tl % 2
                    P.task("dve", lambda e, tl=tl: e.tensor_tensor(h2[:, tl, :], pp[tl][:], h2[:, tl, :], op=ALU.add),
                           reads=[ppb[tl][0], ppb[tl][1]], writes=[Bw["h2"][tl]])
                    P.task("act", lambda e, sl=sl, tl=tl, srow=srow: e.activation(
                        out=hs[sl][:], in_=h2[:, tl, :], func=AF.Square, accum_out=srow[:, tl:tl + 1]),
                        reads=[Bw["h2"][tl]], writes=[Bw["hs"][sl], bst])
                rstd_from_ssq(srow[:, 0:4], srow[:, 4:8], D, bst, 128, 4)
                for tl in range(4):
                    t = 4 * b + tl
                    sl = tl % 2
                    P.task("dve", lambda e, tl=tl, srow=srow: e.scalar_tensor_tensor(
                        h2[:, tl, :], h2[:, tl, :], srow[:, 4 + tl:5 + tl], gfin[:], op0=ALU.mult, op1=ALU.mult),
                        reads=[bst, B_const], writes=[Bw["h2"][tl]])
                    P.dma("sp", Bt["dout"][sl], lambda e, tl=tl, t=t: e.dma_start(out=out[seq, t * 128:(t + 1) * 128, :], in_=h2[:, tl, :]),
                          reads=[Bw["h2"][tl]])

        def allocA(stack, meta_pass):
            A = {}
            A["w_in_sb"] = sbuf(stack, "w_in_sb", [128, 8, INW], BF16)
            A["xt"] = [sbuf(stack, "xtA%d" % i, [128, D], F32) for i in range(4)]
            A["xs"] = [sbuf(stack, "xsA%d" % i, [128, D], BF16) for i in range(2)]
            A["hnT"] = sbuf(stack, "hnT", [128, 8, 512], BF16)
            A["th"] = [sbuf(stack, "th%d" % i, [128, 512], F32) for i in range(2)]
            A["ust"] = [sbuf(stack, "ust%d" % i, [128, 512], F32) for i in range(2)]
            A["Bust"] = [Buf(), Buf()]
            A["dust"] = DS["dust"]
            A["cn"] = [sbuf(stack, "cn%d" % i, [128, 512], BF16) for i in range(2)]
            A["ckvT"] = sbuf(stack, "ckvT", [128, 2, 128], BF16)
            A["ktm"] = sbuf(stack, "ktm", [128, NH, DN + DR], BF16)
            A["krt"] = sbuf(stack, "krt", [128, 96], F32)
            A["Bx"] = [Buf() for _ in range(4)]
            A["Bxs"] = [Buf(), Buf()]
            A["BhnT"] = [Buf() for _ in range(4)]
            A["Bth"] = [Buf(), Buf()]
            A["Bup"] = [[Buf() for _ in range(4)] for _ in range(NB)]
            A["Bup_pad"] = Buf()
            A["Bcn"] = [Buf(), Buf()]
            A["BckvT"] = Buf()
            A["Bktm_n"] = Buf()
            A["Bktm_r"] = Buf()
            A["Bkrt"] = Buf()
            A["Bwin"] = Buf()
            A["dx"] = DS["dxA"]
            A["dwin"] = DS["dwin"]
            if not meta_pass:
                A["C"] = C_all
                C_all["csq"], C_all["Bcsq"] = A["th"], A["Bth"]
            return A

        def allocB(stack):
            Bt = {}
            Bt["QT"] = sbuf(stack, "QT", [128, NH, 512], BF16)
            Bt["qtm"] = [sbuf(stack, "qtm%d" % i, [128, NH, DN + DR], BF16) for i in range(2)]
            Bt["qrt"] = sbuf(stack, "qrt", [128, NH, 64], F32)
            Bt["pT"] = [sbuf(stack, "pT%d" % i, [128, 1024], BF16) for i in range(4)]
            Bt["onT"] = sbuf(stack, "onT", [128, 4, 512], BF16)
            Bt["bcs"] = sbuf(stack, "bcs", [128, 512], F32)
            Bt["rs"] = Bt["bcs"]
            Bt["h2"] = sbuf(stack, "h2", [128, 4, D], F32)
            Bt["hs"] = [sbuf(stack, "hs%d" % i, [128, D], BF16) for i in range(2)]
            Bt["hfT"] = sbuf(stack, "hfT", [128, 8, 512], BF16)
            Bt["wgu"] = [sbuf(stack, "wgu%d" % i, [128, 2, 1024], BF16) for i in range(2)]
            Bt["wdb"] = [sbuf(stack, "wdb%d" % i, [128, D], BF16) for i in range(3)]
            Bt["C"] = None
            Bt["sg"] = [sbuf(stack, "sg%d" % i, [128, 512], F32) for i in range(2)]
            Bt["aT"] = sbuf(stack, "aT", [128, NJ, 512], BF16)
            Bt["w_out_sb"] = Bt["aT"][:, 0:16, :].rearrange("p a b -> p (a b)").rearrange("p (k n) -> p k n", k=8)
            Bt["Bw"] = {
                "w_out": Buf(), "QT": [Buf() for _ in range(4)], "qtm_n": [Buf(), Buf()], "qtm_r": [Buf(), Buf()],
                "qrt": Buf(), "pT": [Buf() for _ in range(4)], "onT": [Buf() for _ in range(NH)], "rs": Buf(), "bcs": Buf(),
                "xt": [Buf(), Buf()], "h2": [Buf() for _ in range(4)], "hs": [Buf(), Buf()], "hfT": [Buf() for _ in range(4)],
                "wgu": [Buf() for _ in range(3)], "wd": [Buf() for _ in range(3)], "sg": [Buf(), Buf()],
                "aT": [Buf() for _ in range(NJ)], "yt": [Buf(), Buf()], "ot": [Buf(), Buf()],
            }
            Bt["C"] = C_all
            C_all["csq"] = Bt["sg"]
            Bt["dx"] = DS["dxB"]
            Bt["dwgu"] = DS["dwgu"]
            Bt["dwd"] = DS["dwd"]
            Bt["dout"] = DS["dout"]
            Bt["dwout"] = DS["dwout"]
            return Bt

        DS = {
            "dxA": [P.new_dma_sem("dxA%d" % i) for i in range(4)], "dwin": P.new_dma_sem("dwin"),
            "dxB": [P.new_dma_sem("dxB0"), P.new_dma_sem("dxB1")],
            "dwgu": [P.new_dma_sem("dwgu%d" % i) for i in range(3)],
            "dwd": [P.new_dma_sem("dwd%d" % i) for i in range(3)],
            "dout": [P.new_dma_sem("dout0"), P.new_dma_sem("dout1")], "dwout": P.new_dma_sem("dwout"),
            "dust": [P.new_dma_sem("dust0"), P.new_dma_sem("dust1")], "duwin": P.new_dma_sem("duwin"), "dupm": P.new_dma_sem("dupm"),
        }
        C_all = allocC(st, "P", csq=[None, None])
        pending0 = []
        with ExitStack() as s1:
            A = allocA(s1, True)
            P.dma("sp", A["dwin"], lambda e: e.dma_start(out=A["w_in_sb"][:].rearrange("p k n -> p (k n)"), in_=win_s),
                  reads=B_win_parts, writes=[A["Bwin"]])
            if stop >= 2 and stop != 3.5:
                phaseA(0, A, True)
            P.barrier()

        for seq in range(NSEQ):
            with ExitStack() as s1:
                A = allocA(s1, False)
                A["load_win"] = lambda A=A: P.dma("sp", A["dwin"], lambda e: e.dma_start(
                    out=A["w_in_sb"][:].rearrange("p k n -> p (k n)"), in_=win_s), reads=B_win_parts, writes=[A["Bwin"]])
                if stop >= 3:
                    phaseA(seq, A, False)
                P.barrier()
            with ExitStack() as s1:
                Bt = allocB(s1)
                if stop >= 5:
                    phaseB(seq, Bt)
                P.barrier()
        P.barrier()
    return nc


def _rope_tables(S):
    inv = (1.0 / (np.float32(10000.0) ** (np.arange(0, DR, 2, dtype=np.float32) / np.float32(DR)))).astype(np.float32)
    pos = np.arange(S + NMETA, dtype=np.float32)
    ang = (pos[:, None] * inv[None, :]).astype(np.float32)
    cos = np.cos(ang).astype(np.float32)
    sin = np.sin(ang).astype(np.float32)
    cs = np.concatenate([cos, cos], axis=1)
    sn = np.concatenate([-sin, sin], axis=1)
    NT = S // 128
    cs_tok = np.ascontiguousarray(cs[NMETA:].reshape(NT, 128, 32).transpose(1, 0, 2))
    sn_tok = np.ascontiguousarray(sn[NMETA:].reshape(NT, 128, 32).transpose(1, 0, 2))
    return cs_tok, sn_tok, np.ascontiguousarray(cs[:NMETA]), np.ascontiguousarray(sn[:NMETA])


def _layout_weights(meta_tokens, attn_norm_g, w_in, q_norm_g, w_q_up, kv_norm_g, w_kv_up, conv_dw_w, conv_dw_b,
                    conv_ln_g, conv_ln_b, w_out, ffn_norm_g, w_gate, w_up, w_down, final_norm_g, S):
    f = lambda a: np.ascontiguousarray(np.asarray(a, dtype=np.float32))
    cs_tok, sn_tok, cs_meta, sn_meta = _rope_tables(S)
    vec = lambda v, k: f(np.asarray(v).reshape(k, 128).T)
    return {
        "meta": f(meta_tokens),
        "w_in_l": f(np.asarray(w_in[0]).reshape(8, 128, INW).transpose(1, 0, 2)),
        "wq_l": f(np.asarray(w_q_up[0]).reshape(2, 128, 768).transpose(1, 0, 2)),
        "wkv_l": f(np.asarray(w_kv_up[0]).reshape(2, 128, 1024).transpose(1, 0, 2)),
        "w_out_l": f(np.asarray(w_out[0]).reshape(8, 128, D).transpose(1, 0, 2)),
        "wg_l": f(np.asarray(w_gate[0]).reshape(8, 128, NJ, 128).transpose(2, 1, 0, 3)),
        "wu_l": f(np.asarray(w_up[0]).reshape(8, 128, NJ, 128).transpose(2, 1, 0, 3)),
        "wd_l": f(np.asarray(w_down[0]).reshape(NJ, 128, D)),
        "gattn_l": vec(attn_norm_g[0], 8),
        "gq_l": vec(q_norm_g[0], 2),
        "gkv_l": vec(kv_norm_g[0], 2),
        "gffn_l": vec(ffn_norm_g[0], 8),
        "cw_l": f(np.asarray(conv_dw_w[0]).T.reshape(4, 128, CK).transpose(1, 0, 2)),
        "cb_l": vec(conv_dw_b[0], 4),
        "lng_l": vec(conv_ln_g[0], 4),
        "lnb_l": vec(conv_ln_b[0], 4),
        "gfin_l": f(np.broadcast_to(np.asarray(final_norm_g)[None, :], (128, D))),
        "ident_l": np.eye(128, dtype=np.float32).astype(ml_dtypes.bfloat16),
        "cs_tok_l": cs_tok, "sn_tok_l": sn_tok, "cs_meta_l": cs_meta, "sn_meta_l": sn_meta,
    }


_NC_CACHE = {}


def kernel(x_prompt, x_sample, meta_tokens, attn_norm_g, w_in, q_norm_g, w_q_up, kv_norm_g, w_kv_up,
           conv_dw_w, conv_dw_b, conv_ln_g, conv_ln_b, w_out, ffn_norm_g, w_gate, w_up, w_down, final_norm_g):
    x_prompt = np.asarray(x_prompt, dtype=np.float32)
    x_sample = np.asarray(x_sample, dtype=np.float32)
    nb_p, S, _ = x_prompt.shape
    nb_s = x_sample.shape[0]
    assert x_sample.shape[1] == S
    ntot = nb_p + nb_s
    assert ntot % N_CORES == 0
    nseq = ntot // N_CORES
    key = (nseq, S)
    if key not in _NC_CACHE:
        _NC_CACHE[key] = build(nseq, S)
    nc = _NC_CACHE[key]
    wl = _layout_weights(meta_tokens, attn_norm_g, w_in, q_norm_g, w_q_up, kv_norm_g, w_kv_up, conv_dw_w, conv_dw_b,
                         conv_ln_g, conv_ln_b, w_out, ffn_norm_g, w_gate, w_up, w_down, final_norm_g, S)
    seqs = [x_prompt[i] for i in range(nb_p)] + [x_sample[i] for i in range(nb_s)]
    in_maps = []
    for c in range(N_CORES):
        m = dict(wl)
        m["x"] = np.ascontiguousarray(np.stack(seqs[c * nseq:(c + 1) * nseq], axis=0))
        in_maps.append(m)
    res = run_bass_kernel_spmd(nc, in_maps, core_ids=list(range(N_CORES)))
    outs = [res.results[c]["out"] for c in range(N_CORES)]
    allo = np.concatenate(outs, axis=0)
    return (np.ascontiguousarray(allo[:nb_p]), np.ascontiguousarray(allo[nb_p:]))
```
